# Optimizing a Trainium2 kernel written in Bass

```python
import jax, jax.numpy as jnp
from jax import lax
import numpy as np

D_MODEL = 1024
BATCH = 16
SEQ = 4096
DEPTH = 4

N_MIXERS = 2
PLE_DIM = 256
CONV_DIM = D_MODEL
CONV_WIDTH = 31
HEAD_DIM = 64
HEADS_PER_GROUP = 8
DILATION_GROUPS = ((128, 1), (512, 4), (2048, 16))
N_GROUPS = len(DILATION_GROUPS)
QKV_DIM = N_GROUPS * HEADS_PER_GROUP * HEAD_DIM
ATTN_OUT_DIM = HEADS_PER_GROUP * HEAD_DIM
BLOCK = 128
ROPE_THETA = 500000.0
ROT_DIM = HEAD_DIM // 4
EPS = 1e-6
NEG_INF = -1e30
N_CONV_LAYERS = (DEPTH + 1) // 2
N_ATTN_LAYERS = DEPTH // 2
OUT_SCALE = (2.0 * DEPTH) ** -0.5
PLE_SCALE = 0.5

kernel_name = "hybrid_conv_dilated_attn_ple"


def rms_norm(x, g):
    x32 = x.astype(jnp.float32)
    y = x32 * lax.rsqrt(jnp.mean(x32 * x32, axis=-1, keepdims=True) + EPS)
    return (y * g.astype(jnp.float32)).astype(x.dtype)


def layer_norm(x, g, b):
    x32 = x.astype(jnp.float32)
    xc = x32 - jnp.mean(x32, axis=-1, keepdims=True)
    y = xc * lax.rsqrt(jnp.mean(xc * xc, axis=-1, keepdims=True) + EPS)
    return (y * g.astype(jnp.float32) + b.astype(jnp.float32)).astype(x.dtype)


def rope_tables(positions):
    inv_freq = 1.0 / (ROPE_THETA ** (jnp.arange(0, ROT_DIM, 2, dtype=jnp.float32) / ROT_DIM))
    ang = positions.astype(jnp.float32)[..., None] * inv_freq
    return jnp.cos(ang), jnp.sin(ang)


def partial_rope(x, cos, sin):
    c = cos[:, :, None, None, :]
    s = sin[:, :, None, None, :]
    half = ROT_DIM // 2
    xr = x[..., :ROT_DIM].astype(jnp.float32)
    x1, x2 = xr[..., :half], xr[..., half:]
    rot = jnp.concatenate([x1 * c - x2 * s, x2 * c + x1 * s], axis=-1).astype(x.dtype)
    return jnp.concatenate([rot, x[..., ROT_DIM:]], axis=-1)


def dilated_band_attention(q, k, v, dilation, band):
    B, S, H, hd = q.shape
    L = S // dilation
    nb = -(-L // BLOCK)
    Lp = nb * BLOCK

    def to_blocks(t):
        t = t.reshape(B, L, dilation, H, hd).transpose(0, 2, 1, 3, 4)
        t = jnp.pad(t, ((0, 0), (0, 0), (0, Lp - L), (0, 0), (0, 0)))
        return t.reshape(B, dilation, nb, BLOCK, H, hd)

    def with_prev(t):
        prev = jnp.pad(t, ((0, 0), (0, 0), (1, 0), (0, 0), (0, 0), (0, 0)))[:, :, :-1]
        return jnp.concatenate([prev, t], axis=3)

    qb, kb, vb = to_blocks(q), to_blocks(k), to_blocks(v)
    kk, vv = with_prev(kb), with_prev(vb)
    s = jnp.einsum('bdnqhc,bdnkhc->bdnhqk', qb, kk,
                   preferred_element_type=jnp.float32) * (hd ** -0.5)
    qi = jnp.arange(BLOCK)[:, None]
    kj = jnp.arange(2 * BLOCK)[None, :]
    dist = BLOCK + qi - kj
    blk = jnp.arange(nb)[:, None, None]
    valid = (dist >= 0) & (dist <= band) & (blk * BLOCK + kj - BLOCK >= 0)
    s = jnp.where(valid[:, None], s, NEG_INF)
    m = jnp.max(s, axis=-1, keepdims=True)
    e = jnp.exp(s - m)
    den = jnp.sum(e, axis=-1, keepdims=True)
    probs = (e / den).astype(v.dtype)
    o = jnp.einsum('bdnhqk,bdnkhc->bdnqhc', probs, vv)
    lse = (m + jnp.log(den))[..., 0]
    o = o.reshape(B, dilation, Lp, H, hd)[:, :, :L].transpose(0, 2, 1, 3, 4).reshape(B, S, H, hd)
    lse = lse.transpose(0, 1, 2, 4, 3).reshape(B, dilation, Lp, H)[:, :, :L]
    lse = lse.transpose(0, 2, 1, 3).reshape(B, S, H)
    return o, lse


def conformer_conv_branch(h, w_in, dw, dw_b, ln_g, ln_b, w_out):
    u = h @ w_in
    a, b, gate = jnp.split(u, [CONV_DIM, 2 * CONV_DIM], axis=-1)
    y = a * jax.nn.sigmoid(b)
    y = lax.conv_general_dilated(
        y, dw[:, None, :].astype(y.dtype), window_strides=(1,),
        padding=((CONV_WIDTH - 1, 0),), dimension_numbers=('NWC', 'WIO', 'NWC'),
        feature_group_count=CONV_DIM) + dw_b
    y = layer_norm(y, ln_g, ln_b)
    y = jax.nn.silu(y) * jax.nn.silu(gate)
    return y @ w_out


def dilated_attention_branch(h, cos, sin, w_in, q_norm, k_norm, w_out):
    B, S, _ = h.shape
    u = h @ w_in
    q, k, v, gate = jnp.split(u, [QKV_DIM, 2 * QKV_DIM, 3 * QKV_DIM], axis=-1)
    shp = (B, S, N_GROUPS, HEADS_PER_GROUP, HEAD_DIM)
    q = partial_rope(rms_norm(q.reshape(shp), q_norm), cos, sin)
    k = partial_rope(rms_norm(k.reshape(shp), k_norm), cos, sin)
    v = v.reshape(shp)
    outs, lses = [], []
    for g, (window, dilation) in enumerate(DILATION_GROUPS):
        o, lse = dilated_band_attention(q[:, :, g], k[:, :, g], v[:, :, g], dilation, window // dilation)
        outs.append(o)
        lses.append(lse)
    wts = jax.nn.softmax(jnp.stack(lses, axis=0), axis=0)
    o = jnp.sum(wts[..., None] * jnp.stack(outs, axis=0).astype(jnp.float32), axis=0).astype(h.dtype)
    y = o.reshape(B, S, ATTN_OUT_DIM) * jax.nn.silu(gate)
    return y @ w_out


def setup_inputs(seed: int = 0) -> dict:
    key = jax.random.key(seed)
    ks = jax.random.split(key, 20)
    f32 = jnp.float32
    nrm = lambda k, shape: jax.random.normal(k, shape, dtype=f32)
    x = nrm(ks[0], (BATCH, SEQ, D_MODEL))
    p = nrm(ks[1], (DEPTH, BATCH, SEQ, PLE_DIM))
    offsets = jax.random.randint(ks[2], (BATCH, 1), 0, 1024, dtype=jnp.int32)
    positions = (jnp.arange(SEQ, dtype=jnp.int32)[None, :] + offsets).astype(jnp.int32)
    norm_g = 1.0 + 0.02 * nrm(ks[3], (DEPTH, D_MODEL))
    conv_w_in = nrm(ks[4], (N_CONV_LAYERS, D_MODEL, 3 * CONV_DIM)) * D_MODEL ** -0.5
    conv_dw = nrm(ks[5], (N_CONV_LAYERS, CONV_WIDTH, CONV_DIM)) * CONV_WIDTH ** -0.5
    conv_dw_b = 0.02 * nrm(ks[6], (N_CONV_LAYERS, CONV_DIM))
    conv_ln_g = 1.0 + 0.02 * nrm(ks[7], (N_CONV_LAYERS, CONV_DIM))
    conv_ln_b = 0.02 * nrm(ks[8], (N_CONV_LAYERS, CONV_DIM))
    conv_w_out = nrm(ks[9], (N_CONV_LAYERS, CONV_DIM, D_MODEL)) * (CONV_DIM ** -0.5 * OUT_SCALE)
    attn_w_in = nrm(ks[10], (N_ATTN_LAYERS, D_MODEL, 3 * QKV_DIM + ATTN_OUT_DIM)) * D_MODEL ** -0.5
    attn_q_norm = 1.0 + 0.02 * nrm(ks[11], (N_ATTN_LAYERS, HEAD_DIM))
    attn_k_norm = 1.0 + 0.02 * nrm(ks[12], (N_ATTN_LAYERS, HEAD_DIM))
    attn_w_out = nrm(ks[13], (N_ATTN_LAYERS, ATTN_OUT_DIM, D_MODEL)) * (ATTN_OUT_DIM ** -0.5 * OUT_SCALE)
    ple_w_proj = nrm(ks[14], (DEPTH, PLE_DIM, D_MODEL)) * (PLE_DIM ** -0.5 * PLE_SCALE)
    ple_norm_g = 1.0 + 0.02 * nrm(ks[15], (DEPTH, D_MODEL))
    ple_w_gate = nrm(ks[16], (DEPTH, D_MODEL, D_MODEL)) * D_MODEL ** -0.5
    return {'x': x, 'p': p, 'positions': positions, 'norm_g': norm_g,
            'conv_w_in': conv_w_in, 'conv_dw': conv_dw, 'conv_dw_b': conv_dw_b,
            'conv_ln_g': conv_ln_g, 'conv_ln_b': conv_ln_b, 'conv_w_out': conv_w_out,
            'attn_w_in': attn_w_in, 'attn_q_norm': attn_q_norm, 'attn_k_norm': attn_k_norm,
            'attn_w_out': attn_w_out, 'ple_w_proj': ple_w_proj, 'ple_norm_g': ple_norm_g,
            'ple_w_gate': ple_w_gate}


def reference(x, p, positions, norm_g, conv_w_in, conv_dw, conv_dw_b, conv_ln_g, conv_ln_b,
              conv_w_out, attn_w_in, attn_q_norm, attn_k_norm, attn_w_out, ple_w_proj,
              ple_norm_g, ple_w_gate):
    cos, sin = rope_tables(positions)
    for i in range(DEPTH):
        h = rms_norm(x, norm_g[i])
        j = i // N_MIXERS
        if i % N_MIXERS == 0:
            x = x + conformer_conv_branch(h, conv_w_in[j], conv_dw[j], conv_dw_b[j],
                                          conv_ln_g[j], conv_ln_b[j], conv_w_out[j])
        else:
            x = x + dilated_attention_branch(h, cos, sin, attn_w_in[j], attn_q_norm[j],
                                             attn_k_norm[j], attn_w_out[j])
        gate = jax.nn.sigmoid(rms_norm(x, ple_norm_g[i]) @ ple_w_gate[i])
        x = x + (p[i] @ ple_w_proj[i]) * gate
    return x
```

```python
import contextlib
import numpy as np
import concourse.bass as bass
import concourse.mybir as mybir
from concourse.bass_utils import run_bass_kernel_spmd

F32 = mybir.dt.float32
BF16 = mybir.dt.bfloat16
I32 = mybir.dt.int32
AF = mybir.ActivationFunctionType
ALU = mybir.AluOpType

D = 1024
CH = 512
EPS = 1e-6
NSLOT = 4
TWO_PI = float(2 * np.pi)
OPT = {'A', 'B', 'E'}


class Prog:
    def __init__(self, nc, stack):
        self.nc = nc
        self.stack = stack
        self.ops = []
        self.same_sync = {'act', 'dve', 'pool'}

    def sb(self, name, shape, dt):
        return self.stack.enter_context(self.nc.sbuf_tensor("sb_" + name, list(shape), dt))

    def ps(self, name, shape, dt=F32):
        return self.stack.enter_context(self.nc.psum_tensor("pp_" + name, list(shape), dt))

    def op(self, eng, fn, r=(), w=(), dma=None):
        r = tuple((k,) if isinstance(k, str) else tuple(k) for k in r)
        w = tuple((k,) if isinstance(k, str) else tuple(k) for k in w)
        self.ops.append((eng, fn, r, w, dma))

    def _deps(self):
        ops = self.ops
        state = {}
        children = {}
        deps = [None] * len(ops)

        def collect(path, is_write, d):
            for k in range(1, len(path) + 1):
                st = state.get(path[:k])
                if st is not None:
                    if st[0] is not None:
                        d.add(st[0])
                    if is_write:
                        d.update(st[1].values())
            stk = [path]
            while stk:
                q = stk.pop()
                for c in children.get(q, ()):
                    st = state.get(c)
                    if st is not None:
                        if st[0] is not None:
                            d.add(st[0])
                        if is_write:
                            d.update(st[1].values())
                    stk.append(c)

        def register(path):
            for k in range(1, len(path)):
                children.setdefault(path[:k], set()).add(path[:k + 1])

        def clear_desc(path):
            stk = [path]
            while stk:
                q = stk.pop()
                for c in children.get(q, ()):
                    state.pop(c, None)
                    stk.append(c)

        for i, (eng, fn, r, w, dma) in enumerate(ops):
            d = set()
            for k in r:
                collect(k, False, d)
            for k in w:
                collect(k, True, d)
            d.discard(i)
            deps[i] = d
            sk = ('dma', dma) if dma is not None else eng
            for k in r:
                register(k)
                st = state.get(k)
                if st is None:
                    st = state[k] = [None, {}]
                st[1][sk] = i
            for k in w:
                register(k)
                clear_desc(k)
                state[k] = [i, {}]
        return deps

    def build(self):
        nc = self.nc
        ops = self.ops
        n = len(ops)
        deps = self._deps()
        target = [False] * n
        for i in range(n):
            eng_i, _, _, _, dma_i = ops[i]
            keep = set()
            for j in deps[i]:
                eng_j, _, _, _, dma_j = ops[j]
                if dma_j is not None:
                    keep.add(j)
                    target[j] = True
                elif eng_j != eng_i or dma_i is not None or eng_j in self.same_sync:
                    keep.add(j)
                    target[j] = True
            deps[i] = keep
        cnt = [0] * n
        semkey = [None] * n
        run = {}
        for i in range(n):
            eng, fn, _, _, dma = ops[i]
            if dma is not None:
                k = ('dma', dma)
                run[k] = run.get(k, 0) + 16
                cnt[i] = run[k]
                semkey[i] = k
                target[i] = True
            elif target[i]:
                assert fn is not None
                run[eng] = run.get(eng, 0) + 1
                cnt[i] = run[eng]
                semkey[i] = eng
        self.max_sem = dict(run)
        known = {}
        clock = [None] * n
        waits = [None] * n
        nwaits = 0
        for i in range(n):
            eng = ops[i][0]
            kn = known.setdefault(eng, {})
            best = {}
            for j in sorted(deps[i], reverse=True):
                k, v = semkey[j], cnt[j]
                if kn.get(k, 0) >= v:
                    continue
                if best.get(k, 0) < v:
                    best[k] = v
                for kk, vv in clock[j].items():
                    if kn.get(kk, 0) < vv:
                        kn[kk] = vv
            waits[i] = [(k, v) for k, v in best.items()]
            nwaits += len(waits[i])
            if target[i]:
                c = dict(kn)
                c[semkey[i]] = cnt[i]
                clock[i] = c
        self.nwaits = nwaits
        sems = {}
        for idx, k in enumerate(run):
            sems[k] = self.stack.enter_context(nc.semaphore("s%d" % idx))
        per_eng = {}
        for i in range(n):
            per_eng.setdefault(ops[i][0], []).append(i)

        def emit(engname, e):
            for i in per_eng.get(engname, []):
                _, fn, _, _, dma = ops[i]
                for k, v in waits[i]:
                    e.wait_ge(sems[k], v)
                if fn is None:
                    continue
                ins = fn(e)
                if target[i]:
                    ins.then_inc(sems[semkey[i]], 16 if dma is not None else 1)

        with nc.Block() as block:
            @block.tensor
            def _(e):
                emit('pe', e)

            @block.scalar
            def _(e):
                emit('act', e)

            @block.vector
            def _(e):
                emit('dve', e)

            @block.gpsimd
            def _(e):
                emit('pool', e)

            @block.sync
            def _(e):
                emit('sp', e)


C_ID, C_BO, C_R, C_MP, C_MC, C_BD, C_BDP, C_BDC, C_COL = 0, 128, 256, 384, 512, 640, 768, 896, 1024
NCST = 1032


def make_consts():
    c = np.zeros((128, NCST), np.float32)
    p = np.arange(128)
    c[:, C_ID:C_ID + 128] = np.eye(128)
    c[:, C_BO:C_BO + 128] = (p[:, None] // 64 == p[None, :] // 64)
    R = np.zeros((128, 128), np.float32)
    for m in range(128):
        mm = m % 64
        if mm < 8:
            R[m + 8, m] = 1.0
        elif mm < 16:
            R[m - 8, m] = 1.0
    c[:, C_R:C_R + 128] = R
    k = p[:, None]
    q = p[None, :]
    c[:, C_MP:C_MP + 128] = (k >= q)
    c[:, C_MC:C_MC + 128] = (k <= q)
    bd = (k // 32 == q // 32)
    c[:, C_BD:C_BD + 128] = bd
    c[:, C_BDP:C_BDP + 128] = bd & (k % 32 >= q % 32)
    c[:, C_BDC:C_BDC + 128] = bd & (k % 32 <= q % 32)
    mm = p % 64
    fi = np.where(mm < 8, mm, mm - 8)
    invf = np.where(mm < 16, 1.0 / (500000.0 ** ((2.0 * fi) / 16.0)), 0.0)
    c[:, C_COL + 0] = (invf.astype(np.float32) / np.float32(TWO_PI)).astype(np.float32)
    c[:, C_COL + 1] = np.where(mm < 8, -1.0, np.where(mm < 16, 1.0, 0.0))
    c[:, C_COL + 2] = EPS
    c[:, C_COL + 3] = invf.astype(np.float32)
    return c


def build_nc(NSEQ=2, S=4096, DEPTH=4):
    nc = bass.Bass("TRN2", target_bir_lowering=False)
    NCHK = S // CH

    def din(name, shape, dt=F32):
        return nc.dram_tensor(name, list(shape), dt, kind="ExternalInput").ap()

    def dscr(name, shape, dt=BF16):
        return nc.dram_tensor(name, list(shape), dt, kind="Internal").ap()

    x_d = din("x", [NSEQ, S, D])
    p_d = din("p", [4, NSEQ, S, 256])
    pos_d = din("positions", [NSEQ, S], I32)
    vec_d = din("vecs", [76, D])
    qkn_d = din("qkn", [128, 8])
    cst_d = din("cst", [128, NCST])
    wci_d = din("conv_w_in", [2, D, 3072])
    wco_d = din("conv_w_out", [2, D, D])
    wai_d = din("attn_w_in", [2, D, 5120])
    wao_d = din("attn_w_out", [2, 512, D])
    wpp_d = din("ple_w_proj", [4, 256, D])
    wpg_d = din("ple_w_gate", [4, D, D])
    out_d = nc.dram_tensor("out", [NSEQ, S, D], F32, kind="ExternalOutput").ap()

    wci_s = dscr("wci_s", [2, D, 3072])
    wco_s = dscr("wco_s", [2, D, D])
    wai_s = dscr("wai_s", [2, D, 5120])
    wao_s = dscr("wao_s", [2, 512, D])
    wpp_s = dscr("wpp_s", [4, 256, D])
    wpg_s = dscr("wpg_s", [4, D, D])
    dgs_s = dscr("dgs_s", [2, 8, 128, 4096])
    kvs_s = dscr("kvs_s", [2, 2, NSEQ, NCHK, 4, 128, 1024])

    with contextlib.ExitStack() as st:
        P = Prog(nc, st)
        op = P.op
        xT = P.sb("xT", [128, 8, CH], F32)
        xin = P.sb("xin", [128, 1, D], F32)
        xo = P.sb("xo", [128, D], F32)
        pin = P.sb("pin", [128, 4, 256], F32)
        pT = P.sb("pT", [128, 2, CH], BF16)
        wring = P.sb("wring", [128, NSLOT, 4096], BF16)
        RA = P.sb("RA", [128, 8, CH], BF16)
        RB = P.sb("RB", [128, 8, CH], BF16)
        RC = P.sb("RC", [128, 8, CH], BF16)
        RD = P.sb("RD", [128, 8, CH], BF16)
        RE = P.sb("RE", [128, 8, CH], F32)
        RF = P.sb("RF", [128, 8, CH + 30], BF16)
        RG = P.sb("RG", [128, 2, 4096], BF16)
        v3T = P.sb("v3T", [128, 4, CH], F32)
        PT = P.sb("PT", [128, 3, CH], BF16)
        st_rt = P.sb("st_rt", [128, CH], F32)
        st_rstd = P.sb("st_rstd", [128, CH], F32)
        st_mean = P.sb("st_mean", [128, CH], F32)
        st_a = P.sb("st_a", [128, CH], F32)
        tmpA = P.sb("tmpA", [128, 2, CH], F32)
        tmpB = P.sb("tmpB", [128, 2, CH], F32)
        sqb = P.sb("sqb", [128, 2, CH], BF16)
        qbb = P.sb("qbb", [128, 2, CH], BF16)
        posi = P.sb("posi", [128, CH], I32)
        Cn = P.sb("Cn", [128, CH], F32)
        Sn = P.sb("Sn", [128, CH], F32)
        k1p = P.sb("k1p", [128, 2, 4, 128], BF16)
        v1p = P.sb("v1p", [128, 2, CH], BF16)
        chist = P.sb("chist", [128, 2, 8, 30], BF16)
        cst = P.sb("cst", [128, NCST], F32)
        cb = P.sb("cb", [128, 1024], BF16)
        m_pc = P.sb("m_pc", [128, CH], BF16)
        m_bd = P.sb("m_bd", [128, CH], BF16)
        m_bdp = P.sb("m_bdp", [128, CH], BF16)
        m_bdc = P.sb("m_bdc", [128, CH], BF16)
        vecT = P.sb("vecT", [128, 8, 76], F32)
        qkn = P.sb("qkn", [128, 8], F32)
        negpi = P.sb("negpi", [128, 1], F32)
        PS = [P.ps("ps%d" % i, [128, CH]) for i in range(8)]

        identf = cst[:, C_ID:C_ID + 128]
        identb = cb[:, C_ID:C_ID + 128]
        blockones = cb[:, C_BO:C_BO + 128]
        Rb = cb[:, C_R:C_R + 128]
        onesb = cb[:, C_BO:C_BO + 64]
        col_turn = cst[:, C_COL + 0:C_COL + 1]
        col_sgn = cst[:, C_COL + 1:C_COL + 2]
        col_eps = cst[:, C_COL + 2:C_COL + 3]
        ones64 = P.sb("ones64", [128, 64], BF16)

        uid = [0]

        def key(base):
            uid[0] += 1
            return (base, uid[0])

        op('sp', lambda e: e.dma_start(out=cst[:], in_=cst_d), w=['cst'], dma='cst')
        vecs = xo[0:76, :]
        op('sp', lambda e: e.dma_start(out=vecs, in_=vec_d), w=['xo'], dma='vecs')
        op('sp', lambda e: e.dma_start(out=qkn[:], in_=qkn_d), w=['qkn'], dma='qkn')
        op('dve', lambda e: e.tensor_copy(out=cb[:], in_=cst[:, 0:1024]), r=['cst'], w=['cb'])
        op('pool', lambda e: e.memset(ones64[:], 1.0), w=['ones64'])
        op('pool', lambda e: e.memset(negpi[:], -float(np.pi)), w=['negpi'])
        for i in range(4):
            src = C_MP if i % 2 == 0 else C_MC
            for (mt, mn, sc_) in ((m_pc, 'm_pc', src), (m_bd, 'm_bd', C_BD), (m_bdp, 'm_bdp', C_BDP), (m_bdc, 'm_bdc', C_BDC)):
                op('dve', lambda e, i=i, mt=mt, sc_=sc_: e.tensor_scalar(out=mt[:, 128 * i:128 * i + 128], in0=cst[:, sc_:sc_ + 128], scalar1=-1.0, scalar2=30000.0, op0=ALU.add, op1=ALU.mult),
                   r=['cst'], w=[(mn, i)])
        cast_keys = []

        def cast(dst, src, rows):
            for r0 in range(0, rows, 128):
                k = key('castk')
                cast_keys.append(k)
                op('pool', lambda e, dst=dst, src=src, r0=r0: e.dma_start(out=dst[r0:r0 + 128, :], in_=src[r0:r0 + 128, :]), w=[k], dma='cast')

        nconv = (DEPTH + 1) // 2
        nattn = DEPTH // 2
        for j in range(nconv):
            cast(wci_s[j], wci_d[j], D)
            cast(wco_s[j], wco_d[j], D)
        for l in range(DEPTH):
            cast(wpg_s[l], wpg_d[l], D)
            cast(wpp_s[l], wpp_d[l], 256)
        for j in range(nattn):
            cast(wai_s[j], wai_d[j], D)
            cast(wao_s[j], wao_d[j], 512)
        op('pool', lambda e: e.nop(), r=cast_keys, w=['wsc_all'])
        for f in range(8):
            op('pe', lambda e, f=f: e.transpose(out=PS[f % 2][:, 0:76], in_=vecs[:, 128 * f:128 * f + 128], identity=identf[0:76, 0:76]),
               r=['xo', 'cst'], w=[('ps', f % 2)])
            op('dve', lambda e, f=f: e.tensor_copy(out=vecT[:, f, :], in_=PS[f % 2][:, 0:76]), r=[('ps', f % 2)], w=[('vecT', f)])
        V_NG, V_PG, V_DWB, V_LNG, V_LNB, V_DW = 0, 4, 8, 10, 12, 14

        def vcol(f, row):
            return vecT[:, f, row:row + 1]

        for j in range(nconv):
            for f in range(8):
                buf = (j * 8 + f) % 2
                dst = RG[:, buf, :].rearrange("p (t m) -> p t m", m=128)
                for t in range(31):
                    eng = 'dve' if t % 2 == 0 else 'pool'
                    op(eng, lambda e, dst=dst, t=t, f=f, j=j: e.tensor_scalar(out=dst[:, t, :], in0=identb, scalar1=vcol(f, V_DW + 31 * j + t), scalar2=None, op0=ALU.mult),
                       r=['cb', ('vecT', f)], w=[('RG', buf, t)])
                op('sp', lambda e, buf=buf, j=j, f=f: e.dma_start(out=dgs_s[j, f][:, 0:31 * 128], in_=RG[:, buf, 0:31 * 128]), r=[('RG', buf)], w=[('dgs', j, f)], dma=('rg', buf))

        slabs = []

        def wview(ap2d, c0, ncols):
            return ap2d.rearrange("(kc p) n -> p kc n", p=128)[:, :, c0:c0 + ncols]

        def layer_slabs(l):
            j = l // 2
            res = []
            if l % 2 == 0:
                for c0 in (0, 1024, 512, 1536, 2048, 2560):
                    res.append((wview(wci_s[j], c0, 512), 8, 512, ['wsc_all']))
                for f in range(8):
                    res.append((dgs_s[j, f].rearrange("p (t m) -> p t m", m=128), 32, 128, [('dgs', j, f)]))
                for c0 in (0, 512):
                    res.append((wview(wco_s[j], c0, 512), 8, 512, ['wsc_all']))
            else:
                for g in range(3):
                    for typ in range(3):
                        res.append((wview(wai_s[j], 1536 * typ + 512 * g, 512), 8, 512, ['wsc_all']))
                res.append((wview(wai_s[j], 4608, 512), 8, 512, ['wsc_all']))
                res.append((wview(wao_s[j], 0, 1024), 4, 1024, ['wsc_all']))
            for c0 in (0, 512):
                res.append((wview(wpg_s[l], c0, 512), 8, 512, ['wsc_all']))
            res.append((wview(wpp_s[l], 0, 1024), 2, 1024, ['wsc_all']))
            return res

        for s_ in range(NSEQ):
            for n_ in range(NCHK):
                for l in range(DEPTH):
                    slabs.extend(layer_slabs(l))
        nslab = len(slabs)
        sl_state = {'next_load': 0, 'next_use': 0}

        def declare_load(k):
            view, A, B, rk = slabs[k]
            slot = k % NSLOT
            dst = wring[:, slot, 0:A * B].rearrange("p (a b) -> p a b", b=B)
            half = A // 2
            op('sp', lambda e, dst=dst, view=view, half=half: e.dma_start(out=dst[:, 0:half, :], in_=view[:, 0:half, :]),
               r=rk, w=[('w', slot, 0)], dma=('w', slot, 0))
            op('sp', lambda e, dst=dst, view=view, half=half, A=A: e.dma_start(out=dst[:, half:A, :], in_=view[:, half:A, :]),
               r=rk, w=[('w', slot, 1)], dma=('w', slot, 1))

        def next_slab(A, B):
            k = sl_state['next_use']
            sl_state['next_use'] += 1
            assert slabs[k][1] == A and slabs[k][2] == B, (k, slabs[k][1:3], A, B)
            while sl_state['next_load'] < min(nslab, NSLOT):
                declare_load(sl_state['next_load'])
                sl_state['next_load'] += 1
            assert k < sl_state['next_load'], "slab %d used before its load could be declared" % k
            slot = k % NSLOT
            view = wring[:, slot, 0:A * B].rearrange("p (a b) -> p a b", b=B)
            half = A // 2

            def wkey(a):
                return ('w', slot, 0 if a < half else 1)
            return view, wkey, k

        def done_slab(sl):
            k = sl[2]
            assert k + NSLOT == sl_state['next_load'] or k + NSLOT >= nslab or True
            m = k + NSLOT
            if m < nslab:
                assert m == sl_state['next_load'], (m, sl_state['next_load'])
                declare_load(m)
                sl_state['next_load'] += 1

        ps_rr = {'i': 0}

        def mm(out, lhsT, rhs, start, stop, r, w, tp=None, sg=False):
            kw = {}
            if tp is not None:
                kw['tile_position'] = tp
            if sg:
                kw['skip_group_check'] = True
            op('pe', lambda e: e.matmul(out, lhsT=lhsT, rhs=rhs, start=start, stop=stop, **kw), r=r, w=w)

        def rms_prep(gain_row, dst, dstkey, bank):
            for f in range(8):
                op('act', lambda e, f=f: e.activation(out=RB[:, f, :], in_=xT[:, f, :], func=AF.Square), r=[('xT', f)], w=[('RB', f)])
            for f in range(8):
                mm(PS[bank][:, :], blockones_all, RB[:, f, :], f == 0, f == 7, r=[('RB', f), 'cb2'], w=[('ps', bank)])
            if 'A' in OPT:
                op('act', lambda e: e.activation(out=st_rt[:], in_=PS[bank][:, :], func=AF.Ln, bias=col_eps, scale=1.0 / D), r=[('ps', bank), 'cst'], w=['st_rt'])
                op('act', lambda e: e.activation(out=st_rstd[:], in_=st_rt[:], func=AF.Exp, scale=-0.5), r=['st_rt'], w=['st_rstd'])
            else:
                op('act', lambda e: e.activation(out=st_rt[:], in_=PS[bank][:, :], func=AF.Sqrt, bias=col_eps, scale=1.0 / D), r=[('ps', bank), 'cst'], w=['st_rt'])
                op('dve', lambda e: e.reciprocal(out=st_rstd[:], in_=st_rt[:]), r=['st_rt'], w=['st_rstd'])
            for f in range(8):
                eng = 'dve'
                op(eng, lambda e, f=f: e.scalar_tensor_tensor(out=dst[:, f, :], in0=xT[:, f, :], scalar=vcol(f, gain_row), in1=st_rstd[:], op0=ALU.mult, op1=ALU.mult),
                   r=[('xT', f), ('vecT', f), 'st_rstd'], w=[(dstkey, f)])

        ones_all = P.sb("ones_all", [128, 128], BF16)
        op('pool', lambda e: e.memset(ones_all[:], 1.0), w=['cb2'])
        blockones_all = ones_all[:, :]

        xin_bufs = ((xin[:, 0, :], [('xin', 0)]), (tmpB[:, :, :].rearrange("p a c -> p (a c)"), [('tmpB', 0), ('tmpB', 1)]))
        xo_bufs = ((xo[:, :], [('xo', 0), ('xo', 1)]), (tmpA[:, :, :].rearrange("p a c -> p (a c)"), [('tmpA', 0), ('tmpA', 1)]))
        pre_dma = set()

        def load_x_dma(s_, n_, blk):
            if (s_, n_, blk) in pre_dma:
                return
            pre_dma.add((s_, n_, blk))
            t0 = n_ * CH
            buf, keys = xin_bufs[blk % 2]
            op('sp', lambda e: e.dma_start(out=buf, in_=x_d[s_, t0 + 128 * blk:t0 + 128 * blk + 128, :]), w=keys, dma=('xin', blk % 2))

        def load_x(s_, n_):
            for blk in range(4):
                load_x_dma(s_, n_, blk)
                if blk + 1 < 4 and blk >= 1:
                    pass
                buf, keys = xin_bufs[blk % 2]
                for half in range(2):
                    bank = 4 + half
                    for ff in range(4):
                        f = 4 * half + ff
                        op('pe', lambda e, f=f, ff=ff, bank=bank, buf=buf: e.transpose(out=PS[bank][:, 128 * ff:128 * ff + 128], in_=buf[:, 128 * f:128 * f + 128], identity=identf),
                           r=keys + ['cst'], w=[('ps', bank)])
                    outap = xT[:, 4 * half:4 * half + 4, 128 * blk:128 * blk + 128]
                    inap = PS[bank][:, :].rearrange("p (f t) -> p f t", t=128)
                    if half == 0:
                        op('act', lambda e, outap=outap, inap=inap: e.copy(out=outap, in_=inap), r=[('ps', bank)], w=[('xT', 4 * half + q) for q in range(4)])
                    else:
                        op('dve', lambda e, outap=outap, inap=inap: e.tensor_copy(out=outap, in_=inap), r=[('ps', bank)], w=[('xT', 4 * half + q) for q in range(4)])

        out_keys = []

        def store_x(s_, n_):
            t0 = n_ * CH
            for blk in range(4):
                buf, keys = xo_bufs[blk % 2]
                for half in range(2):
                    bank = 6 + half
                    for ff in range(4):
                        f = 4 * half + ff
                        op('pe', lambda e, f=f, ff=ff, bank=bank, blk=blk: e.transpose(out=PS[bank][:, 128 * ff:128 * ff + 128], in_=xT[:, f, 128 * blk:128 * blk + 128], identity=identf),
                           r=[('xT', f), 'cst'], w=[('ps', bank)])
                    if half == 0:
                        op('act', lambda e, bank=bank, buf=buf: e.copy(out=buf[:, 0:512], in_=PS[bank][:, :]), r=[('ps', bank)], w=[keys[0]])
                    else:
                        op('dve', lambda e, bank=bank, buf=buf: e.tensor_copy(out=buf[:, 512:1024], in_=PS[bank][:, :]), r=[('ps', bank)], w=[keys[1]])
                k = key('outk')
                out_keys.append(k)
                op('sp', lambda e, blk=blk, buf=buf: e.dma_start(out=out_d[s_, t0 + 128 * blk:t0 + 128 * blk + 128, :], in_=buf), r=keys, w=[k], dma=('xo', blk % 2))

        def ple_pre(l, s_, n_):
            t0 = n_ * CH
            op('sp', lambda e: e.dma_start(out=pin[:], in_=p_d[l, s_, t0:t0 + CH, :].rearrange("(b p) c -> p b c", p=128)), w=['pin'], dma='pin')
            for kf in range(2):
                bank = 4 + kf
                for blk in range(4):
                    op('pe', lambda e, kf=kf, blk=blk, bank=bank: e.transpose(out=PS[bank][:, 128 * blk:128 * blk + 128], in_=pin[:, blk, 128 * kf:128 * kf + 128], identity=identf),
                       r=['pin', 'cst'], w=[('ps', bank)])
                op('act', lambda e, kf=kf, bank=bank: e.copy(out=pT[:, kf, :], in_=PS[bank][:, :]), r=[('ps', bank)], w=[('pT', kf)])

        def ple_full(l, s_, n_):
            rms_prep(V_PG + l, RA, 'RA', 6)
            for hf in range(2):
                sl = next_slab(8, 512)
                wv, wk = sl[0], sl[1]
                for q4 in range(4):
                    nf = 4 * hf + q4
                    bank = nf % 4
                    c0 = 128 * q4
                    for kf in range(8):
                        mm(PS[bank][:, :], wv[:, kf, c0:c0 + 128], RA[:, kf, :], kf == 0, kf == 7, r=[wk(kf), ('RA', kf)], w=[('ps', bank)])
                    op('act', lambda e, bank=bank, nf=nf: e.activation(out=RE[:, nf, :], in_=PS[bank][:, :], func=AF.Sigmoid), r=[('ps', bank)], w=[('RE', nf)])
                done_slab(sl)
            sl = next_slab(2, 1024)
            wv, wk = sl[0], sl[1]
            for nf in range(8):
                bank = 4 + nf % 2
                for kf in range(2):
                    mm(PS[bank][:, :], wv[:, kf, 128 * nf:128 * nf + 128], pT[:, kf, :], kf == 0, kf == 1, r=[wk(kf), ('pT', kf)], w=[('ps', bank)])
                op('dve', lambda e, bank=bank, nf=nf: e.tensor_tensor(out=RE[:, nf, :], in0=PS[bank][:, :], in1=RE[:, nf, :], op=ALU.mult), r=[('ps', bank), ('RE', nf)], w=[('RE', nf)])
                op('pool', lambda e, nf=nf: e.tensor_tensor(out=xT[:, nf, :], in0=xT[:, nf, :], in1=RE[:, nf, :], op=ALU.add), r=[('xT', nf), ('RE', nf)], w=[('xT', nf)])
            done_slab(sl)

        RF_ALL = [('RF', q_) for q_ in range(8)]

        def conv_layer(l, s_, n_):
            j = l // 2
            ybuf = RF
            rms_prep(V_NG + l, RA, 'RA', 6)
            if n_ == 0:
                op('pool', lambda e: e.memset(ybuf[:, :, 0:30], 0.0), w=RF_ALL)
            else:
                op('pool', lambda e: e.tensor_copy(out=ybuf[:, :, 0:30], in_=chist[:, j, :, :]), r=[('chist', j)], w=RF_ALL)
            for hf in range(2):
                sa = next_slab(8, 512)
                sb_ = next_slab(8, 512)
                for q4 in range(4):
                    f = 4 * hf + q4
                    c0 = 128 * q4
                    ba, bb = 0 + (f % 2), 2 + (f % 2)
                    for kf in range(8):
                        mm(PS[ba][:, :], sa[0][:, kf, c0:c0 + 128], RA[:, kf, :], kf == 0, kf == 7, r=[sa[1](kf), ('RA', kf)], w=[('ps', ba)])
                    for kf in range(8):
                        mm(PS[bb][:, :], sb_[0][:, kf, c0:c0 + 128], RA[:, kf, :], kf == 0, kf == 7, r=[sb_[1](kf), ('RA', kf)], w=[('ps', bb)])
                    op('act', lambda e, bb=bb, f=f: e.activation(out=tmpA[:, f % 2, :], in_=PS[bb][:, :], func=AF.Sigmoid), r=[('ps', bb)], w=[('tmpA', f % 2)])
                    op('dve', lambda e, ba=ba, f=f: e.tensor_tensor(out=ybuf[:, f, 30:30 + CH], in0=PS[ba][:, :], in1=tmpA[:, f % 2, :], op=ALU.mult),
                       r=[('ps', ba), ('tmpA', f % 2)], w=[('RF', f)])
                done_slab(sa)
                done_slab(sb_)
            op('pool', lambda e: e.tensor_copy(out=chist[:, j, :, :], in_=ybuf[:, :, CH:CH + 30]), r=RF_ALL, w=[('chist', j)])
            for hf in range(2):
                sg = next_slab(8, 512)
                for q4 in range(4):
                    f = 4 * hf + q4
                    c0 = 128 * q4
                    bank = 4 + f % 2
                    for kf in range(8):
                        mm(PS[bank][:, :], sg[0][:, kf, c0:c0 + 128], RA[:, kf, :], kf == 0, kf == 7, r=[sg[1](kf), ('RA', kf)], w=[('ps', bank)])
                    op('act', lambda e, bank=bank, f=f: e.activation(out=RC[:, f, :], in_=PS[bank][:, :], func=AF.Silu), r=[('ps', bank)], w=[('RC', f)])
                done_slab(sg)
            for f in range(8):
                sd = next_slab(32, 128)
                bank = f % 2
                for t in range(31):
                    mm(PS[bank][:, :], sd[0][:, t, :], ybuf[:, f, t:t + CH], t == 0, t == 30, r=[sd[1](t), ('RF', f)], w=[('ps', bank)])
                done_slab(sd)
                op('act', lambda e, bank=bank, f=f: e.activation(out=RE[:, f, :], in_=PS[bank][:, :], func=AF.Identity, bias=vcol(f, V_DWB + j), scale=1.0),
                   r=[('ps', bank), ('vecT', f)], w=[('RE', f)])
                op('act', lambda e, bank=bank, f=f: e.activation(out=RD[:, f, :], in_=PS[bank][:, :], func=AF.Identity, bias=vcol(f, V_DWB + j), scale=1.0),
                   r=[('ps', bank), ('vecT', f)], w=[('RD', f)])
                op('act', lambda e, bank=bank, f=f: e.activation(out=RB[:, f, :], in_=PS[bank][:, :], func=AF.Square, bias=vcol(f, V_DWB + j), scale=1.0),
                   r=[('ps', bank), ('vecT', f)], w=[('RB', f)])
            for f in range(8):
                mm(PS[6][:, :], blockones_all, RD[:, f, :], f == 0, f == 7, r=[('RD', f), 'cb2'], w=[('ps', 6)])
            for f in range(8):
                mm(PS[7][:, :], blockones_all, RB[:, f, :], f == 0, f == 7, r=[('RB', f), 'cb2'], w=[('ps', 7)])
            op('dve', lambda e: e.tensor_scalar(out=st_mean[:], in0=PS[6][:, :], scalar1=1.0 / D, scalar2=None, op0=ALU.mult), r=[('ps', 6)], w=['st_mean'])
            op('dve', lambda e: e.tensor_tensor(out=st_a[:], in0=st_mean[:], in1=st_mean[:], op=ALU.mult), r=['st_mean'], w=['st_a'])
            op('dve', lambda e: e.scalar_tensor_tensor(out=st_a[:], in0=PS[7][:, :], scalar=1.0 / D, in1=st_a[:], op0=ALU.mult, op1=ALU.subtract), r=[('ps', 7), 'st_a'], w=['st_a'])
            if 'B' in OPT:
                op('act', lambda e: e.activation(out=st_rt[:], in_=st_a[:], func=AF.Ln, bias=col_eps, scale=1.0), r=['st_a', 'cst'], w=['st_rt'])
                op('act', lambda e: e.activation(out=st_rstd[:], in_=st_rt[:], func=AF.Exp, scale=-0.5), r=['st_rt'], w=['st_rstd'])
            else:
                op('act', lambda e: e.activation(out=st_rt[:], in_=st_a[:], func=AF.Sqrt, bias=col_eps, scale=1.0), r=['st_a', 'cst'], w=['st_rt'])
                op('dve', lambda e: e.reciprocal(out=st_rstd[:], in_=st_rt[:]), r=['st_rt'], w=['st_rstd'])
            for f in range(8):
                b2 = f % 2
                op('dve', lambda e, f=f, b2=b2: e.tensor_tensor(out=tmpA[:, b2, :], in0=RE[:, f, :], in1=st_mean[:], op=ALU.subtract), r=[('RE', f), 'st_mean'], w=[('tmpA', b2)])
                op('pool', lambda e, f=f, b2=b2: e.tensor_tensor(out=tmpA[:, b2, :], in0=tmpA[:, b2, :], in1=st_rstd[:], op=ALU.mult), r=[('tmpA', b2), 'st_rstd'], w=[('tmpA', b2)])
                op('act', lambda e, f=f, b2=b2: e.activation(out=tmpB[:, b2, :], in_=tmpA[:, b2, :], func=AF.Silu, bias=vcol(f, V_LNB + j), scale=vcol(f, V_LNG + j)),
                   r=[('tmpA', b2), ('vecT', f)], w=[('tmpB', b2)])
                op('dve', lambda e, f=f, b2=b2: e.tensor_tensor(out=RA[:, f, :], in0=tmpB[:, b2, :], in1=RC[:, f, :], op=ALU.mult), r=[('tmpB', b2), ('RC', f)], w=[('RA', f)])
            for hf in range(2):
                so = next_slab(8, 512)
                for q4 in range(4):
                    nf = 4 * hf + q4
                    c0 = 128 * q4
                    bank = 2 + nf % 2
                    for kf in range(8):
                        mm(PS[bank][:, :], so[0][:, kf, c0:c0 + 128], RA[:, kf, :], kf == 0, kf == 7, r=[so[1](kf), ('RA', kf)], w=[('ps', bank)])
                    op('dve', lambda e, nf=nf, bank=bank: e.tensor_tensor(out=xT[:, nf, :], in0=xT[:, nf, :], in1=PS[bank][:, :], op=ALU.add), r=[('xT', nf), ('ps', bank)], w=[('xT', nf)])
                done_slab(so)

        def rope_tables(s_, n_):
            t0 = n_ * CH
            op('sp', lambda e: e.dma_start(out=posi[:], in_=pos_d[s_:s_ + 1, t0:t0 + CH].partition_broadcast(128)), w=['posi'], dma='posi')
            ang = tmpA[:, 0, :]
            kk = tmpB[:, 0, :]
            ki = posi
            op('dve', lambda e: e.tensor_copy(out=ang, in_=posi[:]), r=['posi'], w=[('tmpA', 0)])
            op('dve', lambda e: e.tensor_scalar(out=ang, in0=ang, scalar1=col_turn, scalar2=None, op0=ALU.mult), r=[('tmpA', 0), 'cst'], w=[('tmpA', 0)])
            for which in range(2):
                dstT = Sn if which == 0 else Cn
                nm = 'Sn' if which == 0 else 'Cn'
                if which == 1:
                    op('dve', lambda e: e.tensor_scalar(out=ang, in0=ang, scalar1=0.25, scalar2=None, op0=ALU.add), r=[('tmpA', 0)], w=[('tmpA', 0)])
                op('dve', lambda e: e.tensor_copy(out=ki[:], in_=ang), r=[('tmpA', 0)], w=['posi'])
                op('dve', lambda e: e.tensor_copy(out=kk, in_=ki[:]), r=['posi'], w=[('tmpB', 0)])
                op('dve', lambda e: e.tensor_tensor(out=kk, in0=ang, in1=kk, op=ALU.subtract), r=[('tmpA', 0), ('tmpB', 0)], w=[('tmpB', 0)])
                op('act', lambda e, dstT=dstT: e.activation(out=dstT[:], in_=kk, func=AF.Sin, scale=TWO_PI), r=[('tmpB', 0)], w=[nm])
            op('dve', lambda e: e.tensor_scalar(out=Sn[:], in0=Sn[:], scalar1=col_sgn, scalar2=None, op0=ALU.mult), r=['Sn', 'cst'], w=['Sn'])

        def perm_view(ap2, g):
            if g == 0:
                return ap2
            if g == 1:
                return ap2.rearrange("p (m r) -> p r m", r=4)
            return ap2.rearrange("p (q r) -> p r q", r=16)

        def grp_view(ap2, g):
            if g == 0:
                return ap2
            if g == 1:
                return ap2.rearrange("p (r m) -> p r m", r=4)
            return ap2.rearrange("p (r q) -> p r q", r=16)

        def inv_view(ap2, g):
            if g == 0:
                return ap2
            if g == 1:
                return ap2.rearrange("p (r m) -> p m r", r=4)
            return ap2.rearrange("p (r q) -> p q r", r=16)

        def nat_view(ap2, g):
            if g == 0:
                return ap2
            if g == 1:
                return ap2.rearrange("p (m r) -> p m r", r=4)
            return ap2.rearrange("p (q r) -> p q r", r=16)

        RGv = RG[:, :, :].rearrange("p b (a c) -> p b a c", c=1024)
        cnt = {'qk': 0, 'hb': 0, 'sc': 0, 'pt': 0}

        hTp = P.sb("hTp", [128, 8, CH], BF16)
        rrt_t = (st_mean, st_a)
        rrt_k = ('st_mean', 'st_a')
        RD2 = P.sb("RD2", [128, 8, CH], BF16)

        def capture(fn):
            saved = P.ops
            P.ops = []
            fn()
            blk = P.ops
            P.ops = saved
            return blk

        def merge_emit(A, B):
            RUN = 100
            A = [sum(A[i:i + RUN], []) for i in range(0, len(A), RUN)]
            na, nb_ = len(A), len(B)
            ia = ib = 0
            while ia < na or ib < nb_:
                if ib >= nb_ or (ia < na and ia * nb_ <= ib * na):
                    P.ops.extend(A[ia])
                    ia += 1
                else:
                    P.ops.extend(B[ib])
                    ib += 1

        def attn_layer(l, s_, n_):
            j = l // 2
            rms_prep(V_NG + l, RA, 'RA', 6)
            accnum = RE[:, 0:4]
            accden = RE[:, 4:8]
            gsT = RF[:, 0:4, 0:CH]
            yT = RF[:, 4:8, 0:CH]

            def kvbuf(g):
                t, nm = (RD, 'RD') if g % 2 == 0 else (RD2, 'RD2')
                return t[:, 0:4], t[:, 4:8], nm, t

            def prep_blocks(g):
                blocks = []
                qb0 = 4 * (g % 2)
                QT = RC[:, qb0:qb0 + 4]
                kcur, vcur, kvn, kvt = kvbuf(g)
                sls = {}
                if g >= 1:
                    def blk():
                        for kf in range(8):
                            op('pool', lambda e, kf=kf: e.tensor_copy(out=grp_view(hTp[:, kf, :], g), in_=perm_view(RA[:, kf, :], g)), r=[('RA', kf)], w=[('hTp', kf)])
                    blocks.append(capture(blk))
                hsrc, hname = (RA, 'RA') if g == 0 else (hTp, 'hTp')
                for which in range(2):
                    for fc in range(4):
                        def blk(which=which, fc=fc):
                            if fc == 0:
                                sls[which] = next_slab(8, 512)
                            sl = sls[which]
                            if which == 0:
                                dstT, dbase, dname, gcol = QT, qb0, 'RC', 4 * j + 0
                            else:
                                dstT, dbase, dname, gcol = kcur, 0, kvn, 4 * j + 2
                            i2 = cnt['qk'] % 2
                            cnt['qk'] += 1
                            bA = 4 + i2
                            for kf in range(8):
                                mm(PS[bA][:, :], sl[0][:, kf, 128 * fc:128 * fc + 128], hsrc[:, kf, :], kf == 0, kf == 7,
                                   r=[sl[1](kf), (hname, kf)], w=[('ps', bA)])
                            op('act', lambda e: e.activation(out=sqb[:, i2, :], in_=PS[bA][:, :], func=AF.Square), r=[('ps', bA)], w=[('sqb', i2)])
                            op('act', lambda e: e.copy(out=qbb[:, i2, :], in_=PS[bA][:, :]), r=[('ps', bA)], w=[('qbb', i2)])
                            bS, bR = 6, 7
                            mm(PS[bS][:, :], blockones, sqb[:, i2, :], True, True, r=[('sqb', i2), 'cb'], w=[('ps', bS)])
                            mm(PS[bR][:, :], Rb, qbb[:, i2, :], True, True, r=[('qbb', i2), 'cb'], w=[('ps', bR)])
                            op('act', lambda e: e.activation(out=rrt_t[i2][:, :], in_=PS[bS][:, :], func=AF.Sqrt, bias=col_eps, scale=1.0 / 64), r=[('ps', bS), 'cst'], w=[rrt_k[i2]])
                            op('dve', lambda e: e.reciprocal(out=rrt_t[i2][:, :], in_=rrt_t[i2][:, :]), r=[rrt_k[i2]], w=[rrt_k[i2]])
                            op('dve', lambda e: e.scalar_tensor_tensor(out=grp_view(tmpA[:, i2, :], g), in0=grp_view(PS[bA][:, :], g), scalar=qkn[:, gcol:gcol + 1],
                                                                       in1=perm_view(Cn[:, :], g), op0=ALU.mult, op1=ALU.mult),
                               r=[('ps', bA), 'qkn', 'Cn'], w=[('tmpA', i2)])
                            op('dve', lambda e: e.scalar_tensor_tensor(out=grp_view(tmpB[:, i2, :], g), in0=grp_view(PS[bR][:, :], g), scalar=qkn[:, gcol + 1:gcol + 2],
                                                                       in1=perm_view(Sn[:, :], g), op0=ALU.mult, op1=ALU.mult),
                               r=[('ps', bR), 'qkn', 'Sn'], w=[('tmpB', i2)])
                            op('pool', lambda e: e.tensor_tensor(out=tmpA[:, i2, :], in0=tmpA[:, i2, :], in1=tmpB[:, i2, :], op=ALU.add), r=[('tmpA', i2), ('tmpB', i2)], w=[('tmpA', i2)])
                            op('pool', lambda e: e.tensor_tensor(out=dstT[:, fc, :], in0=tmpA[:, i2, :], in1=rrt_t[i2][:, :], op=ALU.mult),
                               r=[('tmpA', i2), rrt_k[i2]], w=[(dname, dbase + fc)])
                            if fc == 3:
                                done_slab(sl)
                        blocks.append(capture(blk))
                vs = {}
                if g < 2:
                    for t in range(4):
                        def blk(t=t):
                            if t == 0:
                                vs['sl'] = next_slab(8, 512)
                            sl = vs['sl']
                            bank = 4 + t % 2
                            for kf in range(8):
                                lhs = hsrc[:, kf, 128 * t:128 * t + 128]
                                mm(PS[bank][:, :], lhs, sl[0][:, kf, :], kf == 0, kf == 7, r=[sl[1](kf), (hname, kf)], w=[('ps', bank)])
                            op('act', lambda e: e.copy(out=vcur[:, t, :], in_=PS[bank][:, :]), r=[('ps', bank)], w=[(kvn, 4 + t)])
                            if t == 3:
                                done_slab(sl)
                        blocks.append(capture(blk))
                else:
                    for fc in range(4):
                        def blk(fc=fc):
                            if fc == 0:
                                vs['sl'] = next_slab(8, 512)
                            sl = vs['sl']
                            bank = 4 + fc % 2
                            for kf in range(8):
                                mm(PS[bank][:, :], sl[0][:, kf, 128 * fc:128 * fc + 128], hsrc[:, kf, :], kf == 0, kf == 7,
                                   r=[sl[1](kf), (hname, kf)], w=[('ps', bank)])
                            op('act', lambda e: e.copy(out=v3T[:, fc, :], in_=PS[bank][:, :]), r=[('ps', bank)], w=[('v3T', fc)])
                            if fc == 3:
                                done_slab(sl)
                        blocks.append(capture(blk))
                    for b in range(4):
                        def blk(b=b):
                            bank = 4 + b % 2
                            for fc in range(4):
                                op('pe', lambda e, fc=fc: e.transpose(out=PS[bank][:, 128 * fc:128 * fc + 128], in_=v3T[:, fc, 128 * b:128 * b + 128], identity=identf),
                                   r=[('v3T', fc), 'cst'], w=[('ps', bank)])
                            op('act', lambda e: e.copy(out=vcur[:, b, :], in_=PS[bank][:, :]), r=[('ps', bank)], w=[(kvn, 4 + b)])
                        blocks.append(capture(blk))
                def blk():
                    if g >= 1 and n_ + 1 < NCHK:
                        dstc = kvs_s[j, g - 1, s_, n_]
                        op('sp', lambda e: e.dma_start(out=dstc[:, :, 0:512].rearrange("f p c -> p f c"), in_=kvt[:, 0:4, :]),
                           r=[(kvn, q_) for q_ in range(4)], w=[('kvs', j, g - 1, s_, n_, 'k')], dma='kvw')
                        for fc in range(4):
                            op('sp', lambda e, fc=fc: e.dma_start(out=dstc[fc, :, 512:1024].rearrange("p (b c) -> p b c", c=128), in_=kvt[:, 4:8, 128 * fc:128 * fc + 128]),
                               r=[(kvn, 4 + q_) for q_ in range(4)], w=[('kvs', j, g - 1, s_, n_, 'v', fc)], dma='kvw')
                blocks.append(capture(blk))
                return blocks

            def core_blocks(g):
                blocks = []
                qb0 = 4 * (g % 2)
                QT = RC[:, qb0:qb0 + 4]
                kcur, vcur, kvn, kvt = kvbuf(g)
                ages = []
                if g >= 1:
                    ages = [a for a in ([1] if g == 1 else [4, 3, 2, 1]) if n_ - a >= 0]

                def load_hist(fc):
                    hb = cnt['hb'] % 2
                    cnt['hb'] += 1
                    res = []
                    for ai, a in enumerate(ages):
                        src = kvs_s[j, g - 1, s_, n_ - a, fc]
                        op('sp', lambda e, hb=hb, ai=ai, src=src: e.dma_start(out=RGv[:, hb, ai, :], in_=src), r=[('kvs', j, g - 1, s_, n_ - a)], w=[('RG', hb, ai)], dma=('rg', hb, ai))
                        res.append((a, RGv[:, hb, ai, 0:512], RGv[:, hb, ai, 512:1024], ('RG', hb, ai)))
                    return res

                hists = {}
                nb, db = 2, 3
                for fc in range(4):
                    started = set()
                    for hh in range(2):
                        def blk(fc=fc, hh=hh, started=started):
                            if hh == 0:
                                if fc == 0 and ages:
                                    hists[0] = load_hist(0)
                                if ages and fc + 1 < 4:
                                    hists[fc + 1] = load_hist(fc + 1)
                            hist = hists.get(fc, [])
                            pr = slice(64 * hh, 64 * hh + 64)
                            batches = []
                            if g == 0:
                                for half in range(2):
                                    prs = []
                                    for tt in range(2):
                                        t = 2 * half + tt
                                        if t == 0:
                                            if n_ > 0:
                                                prs.append((2 * tt, k1p[:, j, fc, :], v1p[:, j, 128 * fc:128 * fc + 128], t, False, [('k1p', j), ('v1p', j)]))
                                        else:
                                            prs.append((2 * tt, kcur[:, fc, 128 * (t - 1):128 * t], vcur[:, t - 1, 128 * fc:128 * fc + 128], t, False, [(kvn, fc), (kvn, 4 + t - 1)]))
                                        prs.append((2 * tt + 1, kcur[:, fc, 128 * t:128 * t + 128], vcur[:, t, 128 * fc:128 * fc + 128], t, True, [(kvn, fc), (kvn, 4 + t)]))
                                    batches.append((m_pc, 'm_pc', prs))
                            elif g == 1:
                                for half in range(2):
                                    prs = []
                                    for tt in range(2):
                                        t = 2 * half + tt
                                        if hist:
                                            a, kv, vv, hk = hist[0]
                                            prs.append((2 * tt, kv[:, 128 * t:128 * t + 128], vv[:, 128 * t:128 * t + 128], t, False, [hk]))
                                        prs.append((2 * tt + 1, kcur[:, fc, 128 * t:128 * t + 128], vcur[:, t, 128 * fc:128 * fc + 128], t, True, [(kvn, fc), (kvn, 4 + t)]))
                                    batches.append((m_pc, 'm_pc', prs))
                            else:
                                seq_ = list(hist) + [(0, None, None, None)]
                                for idx, (a, kv, vv, hk) in enumerate(seq_):
                                    prs = []
                                    for b in range(4):
                                        if a == 0:
                                            prs.append((b, kcur[:, fc, 128 * b:128 * b + 128], vcur[:, b, 128 * fc:128 * fc + 128], b, True, [(kvn, fc), (kvn, 4 + b)]))
                                        else:
                                            prs.append((b, kv[:, 128 * b:128 * b + 128], vv[:, 128 * b:128 * b + 128], b, False, [hk]))
                                    if a == 4:
                                        batches.append((m_bdp, 'm_bdp', prs))
                                    elif a == 0:
                                        batches.append((m_bdc, 'm_bdc', prs))
                                    else:
                                        batches.append((m_bd, 'm_bd', prs))

                            def scores(bi):
                                mk, mkn, prs = batches[bi]
                                sb_ = cnt['sc'] % 2
                                cnt['sc'] += 1
                                mm(PS[sb_][:, :], identb, mk[:, :], True, False, r=['cb', (mkn,)], w=[('ps', sb_)], sg=True)
                                for ii, (slot, kview, vview, qt, last, rk) in enumerate(prs):
                                    mm(PS[sb_][:, 128 * slot:128 * slot + 128], kview[pr, :], QT[pr, fc, 128 * qt:128 * qt + 128], False, ii == len(prs) - 1,
                                       r=rk + [('RC', qb0 + fc)], w=[('ps', sb_)], sg=True)
                                pb = cnt['pt'] % 3
                                cnt['pt'] += 1
                                op('act', lambda e: e.activation(out=PT[:, pb, :], in_=PS[sb_][:, :], func=AF.Exp, scale=0.125), r=[('ps', sb_)], w=[('PT', pb)])
                                return pb

                            def pv(bi, pb):
                                mk, mkn, prs = batches[bi]
                                tp = (0, 64) if hh == 1 else None
                                for (slot, kview, vview, qt, last, rk) in prs:
                                    st1 = hh not in started
                                    started.add(hh)
                                    mm(PS[nb][pr, 128 * qt:128 * qt + 128], vview[:, 64 * hh:64 * hh + 64], PT[:, pb, 128 * slot:128 * slot + 128], st1, last,
                                       r=rk + [('PT', pb)], w=[('ps', nb, hh)], tp=tp, sg=True)
                                    mm(PS[db][pr, 128 * qt:128 * qt + 128], ones64[:, :], PT[:, pb, 128 * slot:128 * slot + 128], st1, last,
                                       r=['ones64', ('PT', pb)], w=[('ps', db, hh)], tp=tp, sg=True)
                            pend = scores(0)
                            for bi in range(len(batches)):
                                nxt = scores(bi + 1) if bi + 1 < len(batches) else None
                                pv(bi, pend)
                                pend = nxt
                            if hh == 1:
                                if g == 0:
                                    op('act', lambda e: e.copy(out=accnum[:, fc, :], in_=PS[nb][:, :]), r=[('ps', nb)], w=[('RE', fc)])
                                    op('dve', lambda e: e.tensor_copy(out=accden[:, fc, :], in_=PS[db][:, :]), r=[('ps', db)], w=[('RE', 4 + fc)])
                                else:
                                    op('dve', lambda e: e.tensor_tensor(out=perm_view(accnum[:, fc, :], g), in0=perm_view(accnum[:, fc, :], g), in1=grp_view(PS[nb][:, :], g), op=ALU.add),
                                       r=[('ps', nb), ('RE', fc)], w=[('RE', fc)])
                                    op('dve', lambda e: e.tensor_tensor(out=perm_view(accden[:, fc, :], g), in0=perm_view(accden[:, fc, :], g), in1=grp_view(PS[db][:, :], g), op=ALU.add),
                                       r=[('ps', db), ('RE', 4 + fc)], w=[('RE', 4 + fc)])
                        blocks.append(capture(blk))
                if g == 0 and n_ + 1 < NCHK:
                    def blk():
                        op('pool', lambda e: e.tensor_copy(out=k1p[:, j, :, :], in_=kcur[:, :, 384:512]), r=[(kvn, q_) for q_ in range(4)], w=[('k1p', j)])
                        op('pool', lambda e: e.tensor_copy(out=v1p[:, j, :], in_=vcur[:, 3, :]), r=[(kvn, 7)], w=[('v1p', j)])
                    blocks.append(capture(blk))
                return blocks

            def gate_blocks():
                blocks = []
                gs = {}
                for fc in range(4):
                    def blk(fc=fc):
                        if fc == 0:
                            gs['sl'] = next_slab(8, 512)
                        sl = gs['sl']
                        bank = 4 + fc % 2
                        for kf in range(8):
                            mm(PS[bank][:, :], sl[0][:, kf, 128 * fc:128 * fc + 128], RA[:, kf, :], kf == 0, kf == 7, r=[sl[1](kf), ('RA', kf)], w=[('ps', bank)])
                        op('act', lambda e: e.activation(out=gsT[:, fc, :], in_=PS[bank][:, :], func=AF.Silu), r=[('ps', bank)], w=[('RF', fc)])
                        if fc == 3:
                            done_slab(sl)
                    blocks.append(capture(blk))
                return blocks

            pb0 = prep_blocks(0)
            for blk_ in pb0:
                P.ops.extend(blk_)
            for g in range(3):
                A = prep_blocks(g + 1) if g < 2 else gate_blocks()
                B = core_blocks(g)
                merge_emit(A, B)
            for fc in range(4):
                if 'E' in OPT:
                    op('act', lambda e, fc=fc: e.activation(out=accden[:, fc, :], in_=accden[:, fc, :], func=AF.Ln), r=[('RE', 4 + fc)], w=[('RE', 4 + fc)])
                    op('act', lambda e, fc=fc: e.activation(out=accden[:, fc, :], in_=accden[:, fc, :], func=AF.Exp, scale=-1.0), r=[('RE', 4 + fc)], w=[('RE', 4 + fc)])
                else:
                    op('dve', lambda e, fc=fc: e.reciprocal(out=accden[:, fc, :], in_=accden[:, fc, :]), r=[('RE', 4 + fc)], w=[('RE', 4 + fc)])
                op('dve', lambda e, fc=fc: e.tensor_tensor(out=accnum[:, fc, :], in0=accnum[:, fc, :], in1=accden[:, fc, :], op=ALU.mult), r=[('RE', fc), ('RE', 4 + fc)], w=[('RE', fc)])
                op('pool', lambda e, fc=fc: e.tensor_tensor(out=yT[:, fc, :], in0=accnum[:, fc, :], in1=gsT[:, fc, :], op=ALU.mult), r=[('RE', fc), ('RF', fc)], w=[('RF', 4 + fc)])
            sl = next_slab(4, 1024)
            for nf in range(8):
                bank = 6 + nf % 2
                for kf in range(4):
                    mm(PS[bank][:, :], sl[0][:, kf, 128 * nf:128 * nf + 128], yT[:, kf, :], kf == 0, kf == 3, r=[sl[1](kf), ('RF', 4 + kf)], w=[('ps', bank)])
                op('dve', lambda e, nf=nf, bank=bank: e.tensor_tensor(out=xT[:, nf, :], in0=xT[:, nf, :], in1=PS[bank][:, :], op=ALU.add), r=[('xT', nf), ('ps', bank)], w=[('xT', nf)])
            done_slab(sl)

        for s_ in range(NSEQ):
            for n_ in range(NCHK):
                load_x(s_, n_)
                if DEPTH > 1:
                    rope_tables(s_, n_)
                for l in range(DEPTH):
                    ple_pre(l, s_, n_)
                    if l % 2 == 0:
                        conv_layer(l, s_, n_)
                    else:
                        attn_layer(l, s_, n_)
                    ple_full(l, s_, n_)
                nxt = (s_, n_ + 1) if n_ + 1 < NCHK else ((s_ + 1, 0) if s_ + 1 < NSEQ else None)
                if nxt is not None:
                    load_x_dma(nxt[0], nxt[1], 0)
                store_x(s_, n_)
        op('sp', None, r=out_keys)
        P.build()
        nc._prog_stats = (len(P.ops), P.nwaits, P.max_sem)
    return nc


_NC_CACHE = {}


def host_inputs(x, p, positions, norm_g, conv_w_in, conv_dw, conv_dw_b, conv_ln_g, conv_ln_b, conv_w_out, attn_w_in,
                attn_q_norm, attn_k_norm, attn_w_out, ple_w_proj, ple_norm_g, ple_w_gate, n_cores, nseq):
    f = lambda a: np.ascontiguousarray(np.asarray(a, dtype=np.float32))
    vecs = np.concatenate([f(norm_g), f(ple_norm_g), f(conv_dw_b), f(conv_ln_g), f(conv_ln_b), f(conv_dw)[0], f(conv_dw)[1]], axis=0)
    assert vecs.shape == (76, D)
    perm = np.arange(64)
    perm[0:8] = np.arange(8, 16)
    perm[8:16] = np.arange(0, 8)
    qn, kn = f(attn_q_norm), f(attn_k_norm)
    qkn = np.zeros((128, 8), np.float32)
    idx = np.arange(128) % 64
    for j in range(2):
        qkn[:, 4 * j + 0] = qn[j][idx]
        qkn[:, 4 * j + 1] = qn[j][perm][idx]
        qkn[:, 4 * j + 2] = kn[j][idx]
        qkn[:, 4 * j + 3] = kn[j][perm][idx]
    cst = make_consts()
    x = f(x)
    p = f(p)
    positions = np.ascontiguousarray(np.asarray(positions, dtype=np.int32))
    shared = {"vecs": vecs, "qkn": qkn, "cst": cst, "conv_w_in": f(conv_w_in), "conv_w_out": f(conv_w_out), "attn_w_in": f(attn_w_in),
              "attn_w_out": f(attn_w_out), "ple_w_proj": f(ple_w_proj), "ple_w_gate": f(ple_w_gate)}
    maps = []
    for c in range(n_cores):
        m = dict(shared)
        m["x"] = np.ascontiguousarray(x[c * nseq:(c + 1) * nseq])
        m["p"] = np.ascontiguousarray(p[:, c * nseq:(c + 1) * nseq])
        m["positions"] = np.ascontiguousarray(positions[c * nseq:(c + 1) * nseq])
        maps.append(m)
    return maps


def kernel(**inputs):
    n_cores = 8
    B, S, _ = inputs["x"].shape
    nseq = B // n_cores
    keyc = (nseq, S, 4)
    if keyc not in _NC_CACHE:
        _NC_CACHE[keyc] = build_nc(nseq, S, 4)
    nc = _NC_CACHE[keyc]
    maps = host_inputs(n_cores=n_cores, nseq=nseq, **inputs)
    res = run_bass_kernel_spmd(nc, maps, core_ids=list(range(n_cores)))
    return np.concatenate([np.asarray(r["out"]) for r in res.results], axis=0).astype(np.float32)
```

```python
import contextlib
import numpy as np
import concourse.bass as bass
import concourse.mybir as mybir
from concourse.bass_utils import run_bass_kernel_spmd

F32 = mybir.dt.float32
BF16 = mybir.dt.bfloat16
I32 = mybir.dt.int32
AF = mybir.ActivationFunctionType
ALU = mybir.AluOpType

D = 1024
CH = 512
EPS = 1e-6
NSLOT = 4
TWO_PI = float(2 * np.pi)
OPT = {'A', 'B', 'E'}


class Prog:
    def __init__(self, nc, stack):
        self.nc = nc
        self.stack = stack
        self.ops = []
        self.same_sync = {'act', 'dve', 'pool'}

    def sb(self, name, shape, dt):
        return self.stack.enter_context(self.nc.sbuf_tensor("sb_" + name, list(shape), dt))

    def ps(self, name, shape, dt=F32):
        return self.stack.enter_context(self.nc.psum_tensor("pp_" + name, list(shape), dt))

    def op(self, eng, fn, r=(), w=(), dma=None):
        r = tuple((k,) if isinstance(k, str) else tuple(k) for k in r)
        w = tuple((k,) if isinstance(k, str) else tuple(k) for k in w)
        self.ops.append((eng, fn, r, w, dma))

    def _deps(self):
        ops = self.ops
        state = {}
        children = {}
        deps = [None] * len(ops)

        def collect(path, is_write, d):
            for k in range(1, len(path) + 1):
                st = state.get(path[:k])
                if st is not None:
                    if st[0] is not None:
                        d.add(st[0])
                    if is_write:
                        d.update(st[1].values())
            stk = [path]
            while stk:
                q = stk.pop()
                for c in children.get(q, ()):
                    st = state.get(c)
                    if st is not None:
                        if st[0] is not None:
                            d.add(st[0])
                        if is_write:
                            d.update(st[1].values())
                    stk.append(c)

        def register(path):
            for k in range(1, len(path)):
                children.setdefault(path[:k], set()).add(path[:k + 1])

        def clear_desc(path):
            stk = [path]
            while stk:
                q = stk.pop()
                for c in children.get(q, ()):
                    state.pop(c, None)
                    stk.append(c)

        for i, (eng, fn, r, w, dma) in enumerate(ops):
            d = set()
            for k in r:
                collect(k, False, d)
            for k in w:
                collect(k, True, d)
            d.discard(i)
            deps[i] = d
            sk = ('dma', dma) if dma is not None else eng
            for k in r:
                register(k)
                st = state.get(k)
                if st is None:
                    st = state[k] = [None, {}]
                st[1][sk] = i
            for k in w:
                register(k)
                clear_desc(k)
                state[k] = [i, {}]
        return deps

    def build(self):
        nc = self.nc
        ops = self.ops
        n = len(ops)
        deps = self._deps()
        target = [False] * n
        for i in range(n):
            eng_i, _, _, _, dma_i = ops[i]
            keep = set()
            for j in deps[i]:
                eng_j, _, _, _, dma_j = ops[j]
                if dma_j is not None:
                    keep.add(j)
                    target[j] = True
                elif eng_j != eng_i or dma_i is not None or eng_j in self.same_sync:
                    keep.add(j)
                    target[j] = True
            deps[i] = keep
        cnt = [0] * n
        semkey = [None] * n
        run = {}
        for i in range(n):
            eng, fn, _, _, dma = ops[i]
            if dma is not None:
                k = ('dma', dma)
                run[k] = run.get(k, 0) + 16
                cnt[i] = run[k]
                semkey[i] = k
                target[i] = True
            elif target[i]:
                assert fn is not None
                run[eng] = run.get(eng, 0) + 1
                cnt[i] = run[eng]
                semkey[i] = eng
        self.max_sem = dict(run)
        known = {}
        clock = [None] * n
        waits = [None] * n
        nwaits = 0
        for i in range(n):
            eng = ops[i][0]
            kn = known.setdefault(eng, {})
            best = {}
            for j in sorted(deps[i], reverse=True):
                k, v = semkey[j], cnt[j]
                if kn.get(k, 0) >= v:
                    continue
                if best.get(k, 0) < v:
                    best[k] = v
                for kk, vv in clock[j].items():
                    if kn.get(kk, 0) < vv:
                        kn[kk] = vv
            waits[i] = [(k, v) for k, v in best.items()]
            nwaits += len(waits[i])
            if target[i]:
                c = dict(kn)
                c[semkey[i]] = cnt[i]
                clock[i] = c
        self.nwaits = nwaits
        sems = {}
        for idx, k in enumerate(run):
            sems[k] = self.stack.enter_context(nc.semaphore("s%d" % idx))
        per_eng = {}
        for i in range(n):
            per_eng.setdefault(ops[i][0], []).append(i)

        def emit(engname, e):
            for i in per_eng.get(engname, []):
                _, fn, _, _, dma = ops[i]
                for k, v in waits[i]:
                    e.wait_ge(sems[k], v)
                if fn is None:
                    continue
                ins = fn(e)
                if target[i]:
                    ins.then_inc(sems[semkey[i]], 16 if dma is not None else 1)

        with nc.Block() as block:
            @block.tensor
            def _(e):
                emit('pe', e)

            @block.scalar
            def _(e):
                emit('act', e)

            @block.vector
            def _(e):
                emit('dve', e)

            @block.gpsimd
            def _(e):
                emit('pool', e)

            @block.sync
            def _(e):
                emit('sp', e)


C_ID, C_BO, C_R, C_MP, C_MC, C_BD, C_BDP, C_BDC, C_COL = 0, 128, 256, 384, 512, 640, 768, 896, 1024
NCST = 1032


def make_consts():
    c = np.zeros((128, NCST), np.float32)
    p = np.arange(128)
    c[:, C_ID:C_ID + 128] = np.eye(128)
    c[:, C_BO:C_BO + 128] = (p[:, None] // 64 == p[None, :] // 64)
    R = np.zeros((128, 128), np.float32)
    for m in range(128):
        mm = m % 64
        if mm < 8:
            R[m + 8, m] = 1.0
        elif mm < 16:
            R[m - 8, m] = 1.0
    c[:, C_R:C_R + 128] = R
    k = p[:, None]
    q = p[None, :]
    c[:, C_MP:C_MP + 128] = (k >= q)
    c[:, C_MC:C_MC + 128] = (k <= q)
    bd = (k // 32 == q // 32)
    c[:, C_BD:C_BD + 128] = bd
    c[:, C_BDP:C_BDP + 128] = bd & (k % 32 >= q % 32)
    c[:, C_BDC:C_BDC + 128] = bd & (k % 32 <= q % 32)
    mm = p % 64
    fi = np.where(mm < 8, mm, mm - 8)
    invf = np.where(mm < 16, 1.0 / (500000.0 ** ((2.0 * fi) / 16.0)), 0.0)
    c[:, C_COL + 0] = (invf.astype(np.float32) / np.float32(TWO_PI)).astype(np.float32)
    c[:, C_COL + 1] = np.where(mm < 8, -1.0, np.where(mm < 16, 1.0, 0.0))
    c[:, C_COL + 2] = EPS
    c[:, C_COL + 3] = invf.astype(np.float32)
    return c


def build_nc(NSEQ=2, S=4096, DEPTH=4):
    nc = bass.Bass("TRN2", target_bir_lowering=False)
    NCHK = S // CH

    def din(name, shape, dt=F32):
        return nc.dram_tensor(name, list(shape), dt, kind="ExternalInput").ap()

    def dscr(name, shape, dt=BF16):
        return nc.dram_tensor(name, list(shape), dt, kind="Internal").ap()

    x_d = din("x", [NSEQ, S, D])
    p_d = din("p", [4, NSEQ, S, 256])
    pos_d = din("positions", [NSEQ, S], I32)
    vec_d = din("vecs", [76, D])
    qkn_d = din("qkn", [128, 8])
    cst_d = din("cst", [128, NCST])
    wci_d = din("conv_w_in", [2, D, 3072])
    wco_d = din("conv_w_out", [2, D, D])
    wai_d = din("attn_w_in", [2, D, 5120])
    wao_d = din("attn_w_out", [2, 512, D])
    wpp_d = din("ple_w_proj", [4, 256, D])
    wpg_d = din("ple_w_gate", [4, D, D])
    out_d = nc.dram_tensor("out", [NSEQ, S, D], F32, kind="ExternalOutput").ap()

    wci_s = dscr("wci_s", [2, D, 3072])
    wco_s = dscr("wco_s", [2, D, D])
    wai_s = dscr("wai_s", [2, D, 5120])
    wao_s = dscr("wao_s", [2, 512, D])
    wpp_s = dscr("wpp_s", [4, 256, D])
    wpg_s = dscr("wpg_s", [4, D, D])
    dgs_s = dscr("dgs_s", [2, 8, 128, 4096])
    kvs_s = dscr("kvs_s", [2, 2, NSEQ, NCHK, 4, 128, 1024])

    with contextlib.ExitStack() as st:
        P = Prog(nc, st)
        op = P.op
        xT = P.sb("xT", [128, 8, CH], F32)
        xin = P.sb("xin", [128, 1, D], F32)
        xo = P.sb("xo", [128, D], F32)
        pin = P.sb("pin", [128, 4, 256], F32)
        pT = P.sb("pT", [128, 2, CH], BF16)
        wring = P.sb("wring", [128, NSLOT, 4096], BF16)
        RA = P.sb("RA", [128, 8, CH], BF16)
        RB = P.sb("RB", [128, 8, CH], BF16)
        RC = P.sb("RC", [128, 8, CH], BF16)
        RD = P.sb("RD", [128, 8, CH], BF16)
        RE = P.sb("RE", [128, 8, CH], F32)
        RF = P.sb("RF", [128, 8, CH + 30], BF16)
        RG = P.sb("RG", [128, 2, 4096], BF16)
        v3T = P.sb("v3T", [128, 4, CH], F32)
        PT = P.sb("PT", [128, 3, CH], BF16)
        st_rt = P.sb("st_rt", [128, CH], F32)
        st_rstd = P.sb("st_rstd", [128, CH], F32)
        st_mean = P.sb("st_mean", [128, CH], F32)
        st_a = P.sb("st_a", [128, CH], F32)
        tmpA = P.sb("tmpA", [128, 2, CH], F32)
        tmpB = P.sb("tmpB", [128, 2, CH], F32)
        sqb = P.sb("sqb", [128, 2, CH], BF16)
        qbb = P.sb("qbb", [128, 2, CH], BF16)
        posi = P.sb("posi", [128, CH], I32)
        Cn = P.sb("Cn", [128, CH], F32)
        Sn = P.sb("Sn", [128, CH], F32)
        k1p = P.sb("k1p", [128, 2, 4, 128], BF16)
        v1p = P.sb("v1p", [128, 2, CH], BF16)
        chist = P.sb("chist", [128, 2, 8, 30], BF16)
        cst = P.sb("cst", [128, NCST], F32)
        cb = P.sb("cb", [128, 1024], BF16)
        m_pc = P.sb("m_pc", [128, CH], BF16)
        m_bd = P.sb("m_bd", [128, CH], BF16)
        m_bdp = P.sb("m_bdp", [128, CH], BF16)
        m_bdc = P.sb("m_bdc", [128, CH], BF16)
        vecT = P.sb("vecT", [128, 8, 76], F32)
        qkn = P.sb("qkn", [128, 8], F32)
        negpi = P.sb("negpi", [128, 1], F32)
        PS = [P.ps("ps%d" % i, [128, CH]) for i in range(8)]

        identf = cst[:, C_ID:C_ID + 128]
        identb = cb[:, C_ID:C_ID + 128]
        blockones = cb[:, C_BO:C_BO + 128]
        Rb = cb[:, C_R:C_R + 128]
        onesb = cb[:, C_BO:C_BO + 64]
        col_turn = cst[:, C_COL + 0:C_COL + 1]
        col_sgn = cst[:, C_COL + 1:C_COL + 2]
        col_eps = cst[:, C_COL + 2:C_COL + 3]
        ones64 = P.sb("ones64", [128, 64], BF16)

        uid = [0]

        def key(base):
            uid[0] += 1
            return (base, uid[0])

        op('sp', lambda e: e.dma_start(out=cst[:], in_=cst_d), w=['cst'], dma='cst')
        vecs = xo[0:76, :]
        op('sp', lambda e: e.dma_start(out=vecs, in_=vec_d), w=['xo'], dma='vecs')
        op('sp', lambda e: e.dma_start(out=qkn[:], in_=qkn_d), w=['qkn'], dma='qkn')
        op('dve', lambda e: e.tensor_copy(out=cb[:], in_=cst[:, 0:1024]), r=['cst'], w=['cb'])
        op('pool', lambda e: e.memset(ones64[:], 1.0), w=['ones64'])
        op('pool', lambda e: e.memset(negpi[:], -float(np.pi)), w=['negpi'])
        for i in range(4):
            src = C_MP if i % 2 == 0 else C_MC
            for (mt, mn, sc_) in ((m_pc, 'm_pc', src), (m_bd, 'm_bd', C_BD), (m_bdp, 'm_bdp', C_BDP), (m_bdc, 'm_bdc', C_BDC)):
                op('dve', lambda e, i=i, mt=mt, sc_=sc_: e.tensor_scalar(out=mt[:, 128 * i:128 * i + 128], in0=cst[:, sc_:sc_ + 128], scalar1=-1.0, scalar2=30000.0, op0=ALU.add, op1=ALU.mult),
                   r=['cst'], w=[(mn, i)])
        nconv = (DEPTH + 1) // 2
        nattn = DEPTH // 2
        for f in range(8):
            op('pe', lambda e, f=f: e.transpose(out=PS[f % 2][:, 0:76], in_=vecs[:, 128 * f:128 * f + 128], identity=identf[0:76, 0:76]),
               r=['xo', 'cst'], w=[('ps', f % 2)])
            op('dve', lambda e, f=f: e.tensor_copy(out=vecT[:, f, :], in_=PS[f % 2][:, 0:76]), r=[('ps', f % 2)], w=[('vecT', f)])
        V_NG, V_PG, V_DWB, V_LNG, V_LNB, V_DW = 0, 4, 8, 10, 12, 14

        def vcol(f, row):
            return vecT[:, f, row:row + 1]

        for j in range(nconv):
            for f in range(8):
                buf = (j * 8 + f) % 2
                dst = RG[:, buf, :].rearrange("p (t m) -> p t m", m=128)
                for t in range(31):
                    eng = 'dve' if t % 2 == 0 else 'pool'
                    op(eng, lambda e, dst=dst, t=t, f=f, j=j: e.tensor_scalar(out=dst[:, t, :], in0=identb, scalar1=vcol(f, V_DW + 31 * j + t), scalar2=None, op0=ALU.mult),
                       r=['cb', ('vecT', f)], w=[('RG', buf, t)])
                op('sp', lambda e, buf=buf, j=j, f=f: e.dma_start(out=dgs_s[j, f][:, 0:31 * 128], in_=RG[:, buf, 0:31 * 128]), r=[('RG', buf)], w=[('dgs', j, f)], dma=('rg', buf))

        cast_keys = {l: [] for l in range(DEPTH)}

        def cast(dst, src, rows, l):
            for r0 in range(0, rows, 128):
                k = key('castk')
                cast_keys[l].append(k)
                op('pool', lambda e, dst=dst, src=src, r0=r0: e.dma_start(out=dst[r0:r0 + 128, :], in_=src[r0:r0 + 128, :]), w=[k], dma='cast%d' % l)

        nconv = (DEPTH + 1) // 2
        nattn = DEPTH // 2
        for l in range(DEPTH):
            j = l // 2
            if l % 2 == 0:
                cast(wci_s[j], wci_d[j], D, l)
                cast(wco_s[j], wco_d[j], D, l)
            else:
                cast(wai_s[j], wai_d[j], D, l)
                cast(wao_s[j], wao_d[j], 512, l)
            cast(wpg_s[l], wpg_d[l], D, l)
            cast(wpp_s[l], wpp_d[l], 256, l)

        slabs = []

        def wview(ap2d, c0, ncols):
            return ap2d.rearrange("(kc p) n -> p kc n", p=128)[:, :, c0:c0 + ncols]

        def layer_slabs(l):
            j = l // 2
            res = []
            if l % 2 == 0:
                for c0 in (0, 1024, 512, 1536, 2048, 2560):
                    res.append((wview(wci_s[j], c0, 512), 8, 512, cast_keys[l]))
                for f in range(8):
                    res.append((dgs_s[j, f].rearrange("p (t m) -> p t m", m=128), 32, 128, [('dgs', j, f)]))
                for c0 in (0, 512):
                    res.append((wview(wco_s[j], c0, 512), 8, 512, cast_keys[l]))
            else:
                for g in range(3):
                    for typ in range(3):
                        res.append((wview(wai_s[j], 1536 * typ + 512 * g, 512), 8, 512, cast_keys[l]))
                res.append((wview(wai_s[j], 4608, 512), 8, 512, cast_keys[l]))
                res.append((wview(wao_s[j], 0, 1024), 4, 1024, cast_keys[l]))
            for c0 in (0, 512):
                res.append((wview(wpg_s[l], c0, 512), 8, 512, cast_keys[l]))
            res.append((wview(wpp_s[l], 0, 1024), 2, 1024, cast_keys[l]))
            return res

        for s_ in range(NSEQ):
            for n_ in range(NCHK):
                for l in range(DEPTH):
                    slabs.extend(layer_slabs(l))
        nslab = len(slabs)
        sl_state = {'next_load': 0, 'next_use': 0}

        def declare_load(k):
            view, A, B, rk = slabs[k]
            slot = k % NSLOT
            dst = wring[:, slot, 0:A * B].rearrange("p (a b) -> p a b", b=B)
            half = A // 2
            op('sp', lambda e, dst=dst, view=view, half=half: e.dma_start(out=dst[:, 0:half, :], in_=view[:, 0:half, :]),
               r=rk, w=[('w', slot, 0)], dma=('w', slot, 0))
            op('sp', lambda e, dst=dst, view=view, half=half, A=A: e.dma_start(out=dst[:, half:A, :], in_=view[:, half:A, :]),
               r=rk, w=[('w', slot, 1)], dma=('w', slot, 1))

        def next_slab(A, B):
            k = sl_state['next_use']
            sl_state['next_use'] += 1
            assert slabs[k][1] == A and slabs[k][2] == B, (k, slabs[k][1:3], A, B)
            while sl_state['next_load'] < min(nslab, NSLOT):
                declare_load(sl_state['next_load'])
                sl_state['next_load'] += 1
            assert k < sl_state['next_load'], "slab %d used before its load could be declared" % k
            slot = k % NSLOT
            view = wring[:, slot, 0:A * B].rearrange("p (a b) -> p a b", b=B)
            half = A // 2

            def wkey(a):
                return ('w', slot, 0 if a < half else 1)
            return view, wkey, k

        def done_slab(sl):
            k = sl[2]
            assert k + NSLOT == sl_state['next_load'] or k + NSLOT >= nslab or True
            m = k + NSLOT
            if m < nslab:
                assert m == sl_state['next_load'], (m, sl_state['next_load'])
                declare_load(m)
                sl_state['next_load'] += 1

        ps_rr = {'i': 0}

        def mm(out, lhsT, rhs, start, stop, r, w, tp=None, sg=False):
            kw = {}
            if tp is not None:
                kw['tile_position'] = tp
            if sg:
                kw['skip_group_check'] = True
            op('pe', lambda e: e.matmul(out, lhsT=lhsT, rhs=rhs, start=start, stop=stop, **kw), r=r, w=w)

        def rms_prep(gain_row, dst, dstkey, bank):
            for f in range(8):
                op('act', lambda e, f=f: e.activation(out=RB[:, f, :], in_=xT[:, f, :], func=AF.Square), r=[('xT', f)], w=[('RB', f)])
            for f in range(8):
                mm(PS[bank][:, :], blockones_all, RB[:, f, :], f == 0, f == 7, r=[('RB', f), 'cb2'], w=[('ps', bank)])
            if 'A' in OPT:
                op('act', lambda e: e.activation(out=st_rt[:], in_=PS[bank][:, :], func=AF.Ln, bias=col_eps, scale=1.0 / D), r=[('ps', bank), 'cst'], w=['st_rt'])
                op('act', lambda e: e.activation(out=st_rstd[:], in_=st_rt[:], func=AF.Exp, scale=-0.5), r=['st_rt'], w=['st_rstd'])
            else:
                op('act', lambda e: e.activation(out=st_rt[:], in_=PS[bank][:, :], func=AF.Sqrt, bias=col_eps, scale=1.0 / D), r=[('ps', bank), 'cst'], w=['st_rt'])
                op('dve', lambda e: e.reciprocal(out=st_rstd[:], in_=st_rt[:]), r=['st_rt'], w=['st_rstd'])
            for f in range(8):
                eng = 'dve'
                op(eng, lambda e, f=f: e.scalar_tensor_tensor(out=dst[:, f, :], in0=xT[:, f, :], scalar=vcol(f, gain_row), in1=st_rstd[:], op0=ALU.mult, op1=ALU.mult),
                   r=[('xT', f), ('vecT', f), 'st_rstd'], w=[(dstkey, f)])

        ones_all = P.sb("ones_all", [128, 128], BF16)
        op('pool', lambda e: e.memset(ones_all[:], 1.0), w=['cb2'])
        blockones_all = ones_all[:, :]

        xin_bufs = ((xin[:, 0, :], [('xin', 0)]), (tmpB[:, :, :].rearrange("p a c -> p (a c)"), [('tmpB', 0), ('tmpB', 1)]))
        xo_bufs = ((xo[:, :], [('xo', 0), ('xo', 1)]), (tmpA[:, :, :].rearrange("p a c -> p (a c)"), [('tmpA', 0), ('tmpA', 1)]))
        pre_dma = set()

        def load_x_dma(s_, n_, blk):
            if (s_, n_, blk) in pre_dma:
                return
            pre_dma.add((s_, n_, blk))
            t0 = n_ * CH
            buf, keys = xin_bufs[blk % 2]
            op('sp', lambda e: e.dma_start(out=buf, in_=x_d[s_, t0 + 128 * blk:t0 + 128 * blk + 128, :]), w=keys, dma=('xin', blk % 2))

        def load_x(s_, n_):
            for blk in range(4):
                load_x_dma(s_, n_, blk)
                if blk + 1 < 4 and blk >= 1:
                    pass
                buf, keys = xin_bufs[blk % 2]
                for half in range(2):
                    bank = 4 + half
                    for ff in range(4):
                        f = 4 * half + ff
                        op('pe', lambda e, f=f, ff=ff, bank=bank, buf=buf: e.transpose(out=PS[bank][:, 128 * ff:128 * ff + 128], in_=buf[:, 128 * f:128 * f + 128], identity=identf),
                           r=keys + ['cst'], w=[('ps', bank)])
                    outap = xT[:, 4 * half:4 * half + 4, 128 * blk:128 * blk + 128]
                    inap = PS[bank][:, :].rearrange("p (f t) -> p f t", t=128)
                    if half == 0:
                        op('act', lambda e, outap=outap, inap=inap: e.copy(out=outap, in_=inap), r=[('ps', bank)], w=[('xT', 4 * half + q) for q in range(4)])
                    else:
                        op('dve', lambda e, outap=outap, inap=inap: e.tensor_copy(out=outap, in_=inap), r=[('ps', bank)], w=[('xT', 4 * half + q) for q in range(4)])

        out_keys = []

        def store_x(s_, n_):
            t0 = n_ * CH
            for blk in range(4):
                buf, keys = xo_bufs[blk % 2]
                for half in range(2):
                    bank = 6 + half
                    for ff in range(4):
                        f = 4 * half + ff
                        op('pe', lambda e, f=f, ff=ff, bank=bank, blk=blk: e.transpose(out=PS[bank][:, 128 * ff:128 * ff + 128], in_=xT[:, f, 128 * blk:128 * blk + 128], identity=identf),
                           r=[('xT', f), 'cst'], w=[('ps', bank)])
                    if half == 0:
                        op('act', lambda e, bank=bank, buf=buf: e.copy(out=buf[:, 0:512], in_=PS[bank][:, :]), r=[('ps', bank)], w=[keys[0]])
                    else:
                        op('dve', lambda e, bank=bank, buf=buf: e.tensor_copy(out=buf[:, 512:1024], in_=PS[bank][:, :]), r=[('ps', bank)], w=[keys[1]])
                k = key('outk')
                out_keys.append(k)
                op('sp', lambda e, blk=blk, buf=buf: e.dma_start(out=out_d[s_, t0 + 128 * blk:t0 + 128 * blk + 128, :], in_=buf), r=keys, w=[k], dma=('xo', blk % 2))

        def ple_pre(l, s_, n_):
            t0 = n_ * CH
            op('sp', lambda e: e.dma_start(out=pin[:], in_=p_d[l, s_, t0:t0 + CH, :].rearrange("(b p) c -> p b c", p=128)), w=['pin'], dma='pin')
            for kf in range(2):
                bank = 4 + kf
                for blk in range(4):
                    op('pe', lambda e, kf=kf, blk=blk, bank=bank: e.transpose(out=PS[bank][:, 128 * blk:128 * blk + 128], in_=pin[:, blk, 128 * kf:128 * kf + 128], identity=identf),
                       r=['pin', 'cst'], w=[('ps', bank)])
                op('act', lambda e, kf=kf, bank=bank: e.copy(out=pT[:, kf, :], in_=PS[bank][:, :]), r=[('ps', bank)], w=[('pT', kf)])

        def ple_full(l, s_, n_):
            rms_prep(V_PG + l, RA, 'RA', 6)
            for hf in range(2):
                sl = next_slab(8, 512)
                wv, wk = sl[0], sl[1]
                for q4 in range(4):
                    nf = 4 * hf + q4
                    bank = nf % 4
                    c0 = 128 * q4
                    for kf in range(8):
                        mm(PS[bank][:, :], wv[:, kf, c0:c0 + 128], RA[:, kf, :], kf == 0, kf == 7, r=[wk(kf), ('RA', kf)], w=[('ps', bank)])
                    op('act', lambda e, bank=bank, nf=nf: e.activation(out=RE[:, nf, :], in_=PS[bank][:, :], func=AF.Sigmoid), r=[('ps', bank)], w=[('RE', nf)])
                done_slab(sl)
            sl = next_slab(2, 1024)
            wv, wk = sl[0], sl[1]
            for nf in range(8):
                bank = 4 + nf % 2
                for kf in range(2):
                    mm(PS[bank][:, :], wv[:, kf, 128 * nf:128 * nf + 128], pT[:, kf, :], kf == 0, kf == 1, r=[wk(kf), ('pT', kf)], w=[('ps', bank)])
                op('dve', lambda e, bank=bank, nf=nf: e.tensor_tensor(out=RE[:, nf, :], in0=PS[bank][:, :], in1=RE[:, nf, :], op=ALU.mult), r=[('ps', bank), ('RE', nf)], w=[('RE', nf)])
                op('pool', lambda e, nf=nf: e.tensor_tensor(out=xT[:, nf, :], in0=xT[:, nf, :], in1=RE[:, nf, :], op=ALU.add), r=[('xT', nf), ('RE', nf)], w=[('xT', nf)])
            done_slab(sl)

        RF_ALL = [('RF', q_) for q_ in range(8)]

        def conv_layer(l, s_, n_):
            j = l // 2
            ybuf = RF
            rms_prep(V_NG + l, RA, 'RA', 6)
            if n_ == 0:
                op('pool', lambda e: e.memset(ybuf[:, :, 0:30], 0.0), w=RF_ALL)
            else:
                op('pool', lambda e: e.tensor_copy(out=ybuf[:, :, 0:30], in_=chist[:, j, :, :]), r=[('chist', j)], w=RF_ALL)
            for hf in range(2):
                sa = next_slab(8, 512)
                sb_ = next_slab(8, 512)
                for q4 in range(4):
                    f = 4 * hf + q4
                    c0 = 128 * q4
                    ba, bb = 0 + (f % 2), 2 + (f % 2)
                    for kf in range(8):
                        mm(PS[ba][:, :], sa[0][:, kf, c0:c0 + 128], RA[:, kf, :], kf == 0, kf == 7, r=[sa[1](kf), ('RA', kf)], w=[('ps', ba)])
                    for kf in range(8):
                        mm(PS[bb][:, :], sb_[0][:, kf, c0:c0 + 128], RA[:, kf, :], kf == 0, kf == 7, r=[sb_[1](kf), ('RA', kf)], w=[('ps', bb)])
                    op('act', lambda e, bb=bb, f=f: e.activation(out=tmpA[:, f % 2, :], in_=PS[bb][:, :], func=AF.Sigmoid), r=[('ps', bb)], w=[('tmpA', f % 2)])
                    op('dve', lambda e, ba=ba, f=f: e.tensor_tensor(out=ybuf[:, f, 30:30 + CH], in0=PS[ba][:, :], in1=tmpA[:, f % 2, :], op=ALU.mult),
                       r=[('ps', ba), ('tmpA', f % 2)], w=[('RF', f)])
                done_slab(sa)
                done_slab(sb_)
            op('pool', lambda e: e.tensor_copy(out=chist[:, j, :, :], in_=ybuf[:, :, CH:CH + 30]), r=RF_ALL, w=[('chist', j)])
            for hf in range(2):
                sg = next_slab(8, 512)
                for q4 in range(4):
                    f = 4 * hf + q4
                    c0 = 128 * q4
                    bank = 4 + f % 2
                    for kf in range(8):
                        mm(PS[bank][:, :], sg[0][:, kf, c0:c0 + 128], RA[:, kf, :], kf == 0, kf == 7, r=[sg[1](kf), ('RA', kf)], w=[('ps', bank)])
                    op('act', lambda e, bank=bank, f=f: e.activation(out=RC[:, f, :], in_=PS[bank][:, :], func=AF.Silu), r=[('ps', bank)], w=[('RC', f)])
                done_slab(sg)
            for f in range(8):
                sd = next_slab(32, 128)
                bank = f % 2
                for t in range(31):
                    mm(PS[bank][:, :], sd[0][:, t, :], ybuf[:, f, t:t + CH], t == 0, t == 30, r=[sd[1](t), ('RF', f)], w=[('ps', bank)])
                done_slab(sd)
                op('act', lambda e, bank=bank, f=f: e.activation(out=RE[:, f, :], in_=PS[bank][:, :], func=AF.Identity, bias=vcol(f, V_DWB + j), scale=1.0),
                   r=[('ps', bank), ('vecT', f)], w=[('RE', f)])
                op('act', lambda e, bank=bank, f=f: e.activation(out=RD[:, f, :], in_=PS[bank][:, :], func=AF.Identity, bias=vcol(f, V_DWB + j), scale=1.0),
                   r=[('ps', bank), ('vecT', f)], w=[('RD', f)])
                op('act', lambda e, bank=bank, f=f: e.activation(out=RB[:, f, :], in_=PS[bank][:, :], func=AF.Square, bias=vcol(f, V_DWB + j), scale=1.0),
                   r=[('ps', bank), ('vecT', f)], w=[('RB', f)])
            for f in range(8):
                mm(PS[6][:, :], blockones_all, RD[:, f, :], f == 0, f == 7, r=[('RD', f), 'cb2'], w=[('ps', 6)])
            for f in range(8):
                mm(PS[7][:, :], blockones_all, RB[:, f, :], f == 0, f == 7, r=[('RB', f), 'cb2'], w=[('ps', 7)])
            op('dve', lambda e: e.tensor_scalar(out=st_mean[:], in0=PS[6][:, :], scalar1=1.0 / D, scalar2=None, op0=ALU.mult), r=[('ps', 6)], w=['st_mean'])
            op('dve', lambda e: e.tensor_tensor(out=st_a[:], in0=st_mean[:], in1=st_mean[:], op=ALU.mult), r=['st_mean'], w=['st_a'])
            op('dve', lambda e: e.scalar_tensor_tensor(out=st_a[:], in0=PS[7][:, :], scalar=1.0 / D, in1=st_a[:], op0=ALU.mult, op1=ALU.subtract), r=[('ps', 7), 'st_a'], w=['st_a'])
            if 'B' in OPT:
                op('act', lambda e: e.activation(out=st_rt[:], in_=st_a[:], func=AF.Ln, bias=col_eps, scale=1.0), r=['st_a', 'cst'], w=['st_rt'])
                op('act', lambda e: e.activation(out=st_rstd[:], in_=st_rt[:], func=AF.Exp, scale=-0.5), r=['st_rt'], w=['st_rstd'])
            else:
                op('act', lambda e: e.activation(out=st_rt[:], in_=st_a[:], func=AF.Sqrt, bias=col_eps, scale=1.0), r=['st_a', 'cst'], w=['st_rt'])
                op('dve', lambda e: e.reciprocal(out=st_rstd[:], in_=st_rt[:]), r=['st_rt'], w=['st_rstd'])
            for f in range(8):
                b2 = f % 2
                op('dve', lambda e, f=f, b2=b2: e.tensor_tensor(out=tmpA[:, b2, :], in0=RE[:, f, :], in1=st_mean[:], op=ALU.subtract), r=[('RE', f), 'st_mean'], w=[('tmpA', b2)])
                op('pool', lambda e, f=f, b2=b2: e.tensor_tensor(out=tmpA[:, b2, :], in0=tmpA[:, b2, :], in1=st_rstd[:], op=ALU.mult), r=[('tmpA', b2), 'st_rstd'], w=[('tmpA', b2)])
                op('act', lambda e, f=f, b2=b2: e.activation(out=tmpB[:, b2, :], in_=tmpA[:, b2, :], func=AF.Silu, bias=vcol(f, V_LNB + j), scale=vcol(f, V_LNG + j)),
                   r=[('tmpA', b2), ('vecT', f)], w=[('tmpB', b2)])
                op('dve', lambda e, f=f, b2=b2: e.tensor_tensor(out=RA[:, f, :], in0=tmpB[:, b2, :], in1=RC[:, f, :], op=ALU.mult), r=[('tmpB', b2), ('RC', f)], w=[('RA', f)])
            for hf in range(2):
                so = next_slab(8, 512)
                for q4 in range(4):
                    nf = 4 * hf + q4
                    c0 = 128 * q4
                    bank = 2 + nf % 2
                    for kf in range(8):
                        mm(PS[bank][:, :], so[0][:, kf, c0:c0 + 128], RA[:, kf, :], kf == 0, kf == 7, r=[so[1](kf), ('RA', kf)], w=[('ps', bank)])
                    op('dve', lambda e, nf=nf, bank=bank: e.tensor_tensor(out=xT[:, nf, :], in0=xT[:, nf, :], in1=PS[bank][:, :], op=ALU.add), r=[('xT', nf), ('ps', bank)], w=[('xT', nf)])
                done_slab(so)

        def rope_tables(s_, n_):
            t0 = n_ * CH
            op('sp', lambda e: e.dma_start(out=posi[:], in_=pos_d[s_:s_ + 1, t0:t0 + CH].partition_broadcast(128)), w=['posi'], dma='posi')
            ang = tmpA[:, 0, :]
            kk = tmpB[:, 0, :]
            ki = posi
            op('dve', lambda e: e.tensor_copy(out=ang, in_=posi[:]), r=['posi'], w=[('tmpA', 0)])
            op('dve', lambda e: e.tensor_scalar(out=ang, in0=ang, scalar1=col_turn, scalar2=None, op0=ALU.mult), r=[('tmpA', 0), 'cst'], w=[('tmpA', 0)])
            for which in range(2):
                dstT = Sn if which == 0 else Cn
                nm = 'Sn' if which == 0 else 'Cn'
                if which == 1:
                    op('dve', lambda e: e.tensor_scalar(out=ang, in0=ang, scalar1=0.25, scalar2=None, op0=ALU.add), r=[('tmpA', 0)], w=[('tmpA', 0)])
                op('dve', lambda e: e.tensor_copy(out=ki[:], in_=ang), r=[('tmpA', 0)], w=['posi'])
                op('dve', lambda e: e.tensor_copy(out=kk, in_=ki[:]), r=['posi'], w=[('tmpB', 0)])
                op('dve', lambda e: e.tensor_tensor(out=kk, in0=ang, in1=kk, op=ALU.subtract), r=[('tmpA', 0), ('tmpB', 0)], w=[('tmpB', 0)])
                op('act', lambda e, dstT=dstT: e.activation(out=dstT[:], in_=kk, func=AF.Sin, scale=TWO_PI), r=[('tmpB', 0)], w=[nm])
            op('dve', lambda e: e.tensor_scalar(out=Sn[:], in0=Sn[:], scalar1=col_sgn, scalar2=None, op0=ALU.mult), r=['Sn', 'cst'], w=['Sn'])

        def perm_view(ap2, g):
            if g == 0:
                return ap2
            if g == 1:
                return ap2.rearrange("p (m r) -> p r m", r=4)
            return ap2.rearrange("p (q r) -> p r q", r=16)

        def grp_view(ap2, g):
            if g == 0:
                return ap2
            if g == 1:
                return ap2.rearrange("p (r m) -> p r m", r=4)
            return ap2.rearrange("p (r q) -> p r q", r=16)

        def inv_view(ap2, g):
            if g == 0:
                return ap2
            if g == 1:
                return ap2.rearrange("p (r m) -> p m r", r=4)
            return ap2.rearrange("p (r q) -> p q r", r=16)

        def nat_view(ap2, g):
            if g == 0:
                return ap2
            if g == 1:
                return ap2.rearrange("p (m r) -> p m r", r=4)
            return ap2.rearrange("p (q r) -> p q r", r=16)

        RGv = RG[:, :, :].rearrange("p b (a c) -> p b a c", c=1024)
        cnt = {'qk': 0, 'hb': 0, 'sc': 0, 'pt': 0}

        hTp = P.sb("hTp", [128, 8, CH], BF16)
        rrt_t = (st_mean, st_a)
        rrt_k = ('st_mean', 'st_a')
        RD2 = P.sb("RD2", [128, 8, CH], BF16)

        def capture(fn):
            saved = P.ops
            P.ops = []
            fn()
            blk = P.ops
            P.ops = saved
            return blk

        def merge_emit(A, B):
            RUN = 9
            A = [sum(A[i:i + RUN], []) for i in range(0, len(A), RUN)]
            na, nb_ = len(A), len(B)
            ia = ib = 0
            while ia < na or ib < nb_:
                if ib >= nb_ or (ia < na and ia * nb_ <= ib * na):
                    P.ops.extend(A[ia])
                    ia += 1
                else:
                    P.ops.extend(B[ib])
                    ib += 1

        def attn_layer(l, s_, n_):
            j = l // 2
            rms_prep(V_NG + l, RA, 'RA', 6)
            accnum = RE[:, 0:4]
            accden = RE[:, 4:8]
            gsT = RF[:, 0:4, 0:CH]
            yT = RF[:, 4:8, 0:CH]

            def kvbuf(g):
                t, nm = (RD, 'RD') if g % 2 == 0 else (RD2, 'RD2')
                return t[:, 0:4], t[:, 4:8], nm, t

            def prep_blocks(g):
                blocks = []
                qb0 = 4 * (g % 2)
                QT = RC[:, qb0:qb0 + 4]
                kcur, vcur, kvn, kvt = kvbuf(g)
                sls = {}
                if g >= 1:
                    def blk():
                        for kf in range(8):
                            op('pool', lambda e, kf=kf: e.tensor_copy(out=grp_view(hTp[:, kf, :], g), in_=perm_view(RA[:, kf, :], g)), r=[('RA', kf)], w=[('hTp', kf)])
                    blocks.append(capture(blk))
                hsrc, hname = (RA, 'RA') if g == 0 else (hTp, 'hTp')
                for which in range(2):
                    for fc in range(4):
                        def blk(which=which, fc=fc):
                            if fc == 0:
                                sls[which] = next_slab(8, 512)
                            sl = sls[which]
                            if which == 0:
                                dstT, dbase, dname, gcol = QT, qb0, 'RC', 4 * j + 0
                            else:
                                dstT, dbase, dname, gcol = kcur, 0, kvn, 4 * j + 2
                            i2 = cnt['qk'] % 2
                            cnt['qk'] += 1
                            bA = 4 + i2
                            for kf in range(8):
                                mm(PS[bA][:, :], sl[0][:, kf, 128 * fc:128 * fc + 128], hsrc[:, kf, :], kf == 0, kf == 7,
                                   r=[sl[1](kf), (hname, kf)], w=[('ps', bA)])
                            op('act', lambda e: e.activation(out=sqb[:, i2, :], in_=PS[bA][:, :], func=AF.Square), r=[('ps', bA)], w=[('sqb', i2)])
                            op('act', lambda e: e.copy(out=qbb[:, i2, :], in_=PS[bA][:, :]), r=[('ps', bA)], w=[('qbb', i2)])
                            bS, bR = 6, 7
                            mm(PS[bS][:, :], blockones, sqb[:, i2, :], True, True, r=[('sqb', i2), 'cb'], w=[('ps', bS)])
                            mm(PS[bR][:, :], Rb, qbb[:, i2, :], True, True, r=[('qbb', i2), 'cb'], w=[('ps', bR)])
                            op('act', lambda e: e.activation(out=rrt_t[i2][:, :], in_=PS[bS][:, :], func=AF.Sqrt, bias=col_eps, scale=1.0 / 64), r=[('ps', bS), 'cst'], w=[rrt_k[i2]])
                            op('dve', lambda e: e.reciprocal(out=rrt_t[i2][:, :], in_=rrt_t[i2][:, :]), r=[rrt_k[i2]], w=[rrt_k[i2]])
                            op('dve', lambda e: e.scalar_tensor_tensor(out=grp_view(tmpA[:, i2, :], g), in0=grp_view(PS[bA][:, :], g), scalar=qkn[:, gcol:gcol + 1],
                                                                       in1=perm_view(Cn[:, :], g), op0=ALU.mult, op1=ALU.mult),
                               r=[('ps', bA), 'qkn', 'Cn'], w=[('tmpA', i2)])
                            op('dve', lambda e: e.scalar_tensor_tensor(out=grp_view(tmpB[:, i2, :], g), in0=grp_view(PS[bR][:, :], g), scalar=qkn[:, gcol + 1:gcol + 2],
                                                                       in1=perm_view(Sn[:, :], g), op0=ALU.mult, op1=ALU.mult),
                               r=[('ps', bR), 'qkn', 'Sn'], w=[('tmpB', i2)])
                            op('pool', lambda e: e.tensor_tensor(out=tmpA[:, i2, :], in0=tmpA[:, i2, :], in1=tmpB[:, i2, :], op=ALU.add), r=[('tmpA', i2), ('tmpB', i2)], w=[('tmpA', i2)])
                            op('pool', lambda e: e.tensor_tensor(out=dstT[:, fc, :], in0=tmpA[:, i2, :], in1=rrt_t[i2][:, :], op=ALU.mult),
                               r=[('tmpA', i2), rrt_k[i2]], w=[(dname, dbase + fc)])
                            if fc == 3:
                                done_slab(sl)
                        blocks.append(capture(blk))
                vs = {}
                if g < 2:
                    for t in range(4):
                        def blk(t=t):
                            if t == 0:
                                vs['sl'] = next_slab(8, 512)
                            sl = vs['sl']
                            bank = 4 + t % 2
                            for kf in range(8):
                                lhs = hsrc[:, kf, 128 * t:128 * t + 128]
                                mm(PS[bank][:, :], lhs, sl[0][:, kf, :], kf == 0, kf == 7, r=[sl[1](kf), (hname, kf)], w=[('ps', bank)])
                            op('act', lambda e: e.copy(out=vcur[:, t, :], in_=PS[bank][:, :]), r=[('ps', bank)], w=[(kvn, 4 + t)])
                            if t == 3:
                                done_slab(sl)
                        blocks.append(capture(blk))
                else:
                    for fc in range(4):
                        def blk(fc=fc):
                            if fc == 0:
                                vs['sl'] = next_slab(8, 512)
                            sl = vs['sl']
                            bank = 4 + fc % 2
                            for kf in range(8):
                                mm(PS[bank][:, :], sl[0][:, kf, 128 * fc:128 * fc + 128], hsrc[:, kf, :], kf == 0, kf == 7,
                                   r=[sl[1](kf), (hname, kf)], w=[('ps', bank)])
                            op('act', lambda e: e.copy(out=v3T[:, fc, :], in_=PS[bank][:, :]), r=[('ps', bank)], w=[('v3T', fc)])
                            if fc == 3:
                                done_slab(sl)
                        blocks.append(capture(blk))
                    for b in range(4):
                        def blk(b=b):
                            bank = 4 + b % 2
                            for fc in range(4):
                                op('pe', lambda e, fc=fc: e.transpose(out=PS[bank][:, 128 * fc:128 * fc + 128], in_=v3T[:, fc, 128 * b:128 * b + 128], identity=identf),
                                   r=[('v3T', fc), 'cst'], w=[('ps', bank)])
                            op('act', lambda e: e.copy(out=vcur[:, b, :], in_=PS[bank][:, :]), r=[('ps', bank)], w=[(kvn, 4 + b)])
                        blocks.append(capture(blk))
                def blk():
                    if g >= 1 and n_ + 1 < NCHK:
                        dstc = kvs_s[j, g - 1, s_, n_]
                        op('sp', lambda e: e.dma_start(out=dstc[:, :, 0:512].rearrange("f p c -> p f c"), in_=kvt[:, 0:4, :]),
                           r=[(kvn, q_) for q_ in range(4)], w=[('kvs', j, g - 1, s_, n_, 'k')], dma='kvw')
                        for fc in range(4):
                            op('sp', lambda e, fc=fc: e.dma_start(out=dstc[fc, :, 512:1024].rearrange("p (b c) -> p b c", c=128), in_=kvt[:, 4:8, 128 * fc:128 * fc + 128]),
                               r=[(kvn, 4 + q_) for q_ in range(4)], w=[('kvs', j, g - 1, s_, n_, 'v', fc)], dma='kvw')
                blocks.append(capture(blk))
                return blocks

            def core_blocks(g):
                blocks = []
                qb0 = 4 * (g % 2)
                QT = RC[:, qb0:qb0 + 4]
                kcur, vcur, kvn, kvt = kvbuf(g)
                ages = []
                if g >= 1:
                    ages = [a for a in ([1] if g == 1 else [4, 3, 2, 1]) if n_ - a >= 0]

                def load_hist(fc):
                    hb = cnt['hb'] % 2
                    cnt['hb'] += 1
                    res = []
                    for ai, a in enumerate(ages):
                        src = kvs_s[j, g - 1, s_, n_ - a, fc]
                        op('sp', lambda e, hb=hb, ai=ai, src=src: e.dma_start(out=RGv[:, hb, ai, :], in_=src), r=[('kvs', j, g - 1, s_, n_ - a)], w=[('RG', hb, ai)], dma=('rg', hb, ai))
                        res.append((a, RGv[:, hb, ai, 0:512], RGv[:, hb, ai, 512:1024], ('RG', hb, ai)))
                    return res

                hists = {}
                nb, db = 2, 3
                for fc in range(4):
                    started = set()
                    for hh in range(2):
                        def blk(fc=fc, hh=hh, started=started):
                            if hh == 0:
                                if fc == 0 and ages:
                                    hists[0] = load_hist(0)
                                if ages and fc + 1 < 4:
                                    hists[fc + 1] = load_hist(fc + 1)
                            hist = hists.get(fc, [])
                            pr = slice(64 * hh, 64 * hh + 64)
                            batches = []
                            if g == 0:
                                for half in range(2):
                                    prs = []
                                    for tt in range(2):
                                        t = 2 * half + tt
                                        if t == 0:
                                            if n_ > 0:
                                                prs.append((2 * tt, k1p[:, j, fc, :], v1p[:, j, 128 * fc:128 * fc + 128], t, False, [('k1p', j), ('v1p', j)]))
                                        else:
                                            prs.append((2 * tt, kcur[:, fc, 128 * (t - 1):128 * t], vcur[:, t - 1, 128 * fc:128 * fc + 128], t, False, [(kvn, fc), (kvn, 4 + t - 1)]))
                                        prs.append((2 * tt + 1, kcur[:, fc, 128 * t:128 * t + 128], vcur[:, t, 128 * fc:128 * fc + 128], t, True, [(kvn, fc), (kvn, 4 + t)]))
                                    batches.append((m_pc, 'm_pc', prs))
                            elif g == 1:
                                for half in range(2):
                                    prs = []
                                    for tt in range(2):
                                        t = 2 * half + tt
                                        if hist:
                                            a, kv, vv, hk = hist[0]
                                            prs.append((2 * tt, kv[:, 128 * t:128 * t + 128], vv[:, 128 * t:128 * t + 128], t, False, [hk]))
                                        prs.append((2 * tt + 1, kcur[:, fc, 128 * t:128 * t + 128], vcur[:, t, 128 * fc:128 * fc + 128], t, True, [(kvn, fc), (kvn, 4 + t)]))
                                    batches.append((m_pc, 'm_pc', prs))
                            else:
                                seq_ = list(hist) + [(0, None, None, None)]
                                for idx, (a, kv, vv, hk) in enumerate(seq_):
                                    prs = []
                                    for b in range(4):
                                        if a == 0:
                                            prs.append((b, kcur[:, fc, 128 * b:128 * b + 128], vcur[:, b, 128 * fc:128 * fc + 128], b, True, [(kvn, fc), (kvn, 4 + b)]))
                                        else:
                                            prs.append((b, kv[:, 128 * b:128 * b + 128], vv[:, 128 * b:128 * b + 128], b, False, [hk]))
                                    if a == 4:
                                        batches.append((m_bdp, 'm_bdp', prs))
                                    elif a == 0:
                                        batches.append((m_bdc, 'm_bdc', prs))
                                    else:
                                        batches.append((m_bd, 'm_bd', prs))

                            def scores(bi):
                                mk, mkn, prs = batches[bi]
                                sb_ = cnt['sc'] % 2
                                cnt['sc'] += 1
                                mm(PS[sb_][:, :], identb, mk[:, :], True, False, r=['cb', (mkn,)], w=[('ps', sb_)], sg=True)
                                for ii, (slot, kview, vview, qt, last, rk) in enumerate(prs):
                                    mm(PS[sb_][:, 128 * slot:128 * slot + 128], kview[pr, :], QT[pr, fc, 128 * qt:128 * qt + 128], False, ii == len(prs) - 1,
                                       r=rk + [('RC', qb0 + fc)], w=[('ps', sb_)], sg=True)
                                pb = cnt['pt'] % 3
                                cnt['pt'] += 1
                                op('act', lambda e: e.activation(out=PT[:, pb, :], in_=PS[sb_][:, :], func=AF.Exp, scale=0.125), r=[('ps', sb_)], w=[('PT', pb)])
                                return pb

                            def pv(bi, pb):
                                mk, mkn, prs = batches[bi]
                                tp = (0, 64) if hh == 1 else None
                                for (slot, kview, vview, qt, last, rk) in prs:
                                    st1 = hh not in started
                                    started.add(hh)
                                    mm(PS[nb][pr, 128 * qt:128 * qt + 128], vview[:, 64 * hh:64 * hh + 64], PT[:, pb, 128 * slot:128 * slot + 128], st1, last,
                                       r=rk + [('PT', pb)], w=[('ps', nb, hh)], tp=tp, sg=True)
                                    mm(PS[db][pr, 128 * qt:128 * qt + 128], ones64[:, :], PT[:, pb, 128 * slot:128 * slot + 128], st1, last,
                                       r=['ones64', ('PT', pb)], w=[('ps', db, hh)], tp=tp, sg=True)
                            pend = scores(0)
                            for bi in range(len(batches)):
                                nxt = scores(bi + 1) if bi + 1 < len(batches) else None
                                pv(bi, pend)
                                pend = nxt
                            if hh == 1:
                                if g == 0:
                                    op('act', lambda e: e.copy(out=accnum[:, fc, :], in_=PS[nb][:, :]), r=[('ps', nb)], w=[('RE', fc)])
                                    op('dve', lambda e: e.tensor_copy(out=accden[:, fc, :], in_=PS[db][:, :]), r=[('ps', db)], w=[('RE', 4 + fc)])
                                else:
                                    op('dve', lambda e: e.tensor_tensor(out=perm_view(accnum[:, fc, :], g), in0=perm_view(accnum[:, fc, :], g), in1=grp_view(PS[nb][:, :], g), op=ALU.add),
                                       r=[('ps', nb), ('RE', fc)], w=[('RE', fc)])
                                    op('dve', lambda e: e.tensor_tensor(out=perm_view(accden[:, fc, :], g), in0=perm_view(accden[:, fc, :], g), in1=grp_view(PS[db][:, :], g), op=ALU.add),
                                       r=[('ps', db), ('RE', 4 + fc)], w=[('RE', 4 + fc)])
                        blocks.append(capture(blk))
                if g == 0 and n_ + 1 < NCHK:
                    def blk():
                        op('pool', lambda e: e.tensor_copy(out=k1p[:, j, :, :], in_=kcur[:, :, 384:512]), r=[(kvn, q_) for q_ in range(4)], w=[('k1p', j)])
                        op('pool', lambda e: e.tensor_copy(out=v1p[:, j, :], in_=vcur[:, 3, :]), r=[(kvn, 7)], w=[('v1p', j)])
                    blocks.append(capture(blk))
                return blocks

            def gate_blocks():
                blocks = []
                gs = {}
                for fc in range(4):
                    def blk(fc=fc):
                        if fc == 0:
                            gs['sl'] = next_slab(8, 512)
                        sl = gs['sl']
                        bank = 4 + fc % 2
                        for kf in range(8):
                            mm(PS[bank][:, :], sl[0][:, kf, 128 * fc:128 * fc + 128], RA[:, kf, :], kf == 0, kf == 7, r=[sl[1](kf), ('RA', kf)], w=[('ps', bank)])
                        op('act', lambda e: e.activation(out=gsT[:, fc, :], in_=PS[bank][:, :], func=AF.Silu), r=[('ps', bank)], w=[('RF', fc)])
                        if fc == 3:
                            done_slab(sl)
                    blocks.append(capture(blk))
                return blocks

            pb0 = prep_blocks(0)
            for blk_ in pb0:
                P.ops.extend(blk_)
            for g in range(3):
                A = prep_blocks(g + 1) if g < 2 else gate_blocks()
                B = core_blocks(g)
                merge_emit(A, B)
            for fc in range(4):
                if 'E' in OPT:
                    op('act', lambda e, fc=fc: e.activation(out=accden[:, fc, :], in_=accden[:, fc, :], func=AF.Ln), r=[('RE', 4 + fc)], w=[('RE', 4 + fc)])
                    op('act', lambda e, fc=fc: e.activation(out=accden[:, fc, :], in_=accden[:, fc, :], func=AF.Exp, scale=-1.0), r=[('RE', 4 + fc)], w=[('RE', 4 + fc)])
                else:
                    op('dve', lambda e, fc=fc: e.reciprocal(out=accden[:, fc, :], in_=accden[:, fc, :]), r=[('RE', 4 + fc)], w=[('RE', 4 + fc)])
                op('dve', lambda e, fc=fc: e.tensor_tensor(out=accnum[:, fc, :], in0=accnum[:, fc, :], in1=accden[:, fc, :], op=ALU.mult), r=[('RE', fc), ('RE', 4 + fc)], w=[('RE', fc)])
                op('pool', lambda e, fc=fc: e.tensor_tensor(out=yT[:, fc, :], in0=accnum[:, fc, :], in1=gsT[:, fc, :], op=ALU.mult), r=[('RE', fc), ('RF', fc)], w=[('RF', 4 + fc)])
            sl = next_slab(4, 1024)
            for nf in range(8):
                bank = 6 + nf % 2
                for kf in range(4):
                    mm(PS[bank][:, :], sl[0][:, kf, 128 * nf:128 * nf + 128], yT[:, kf, :], kf == 0, kf == 3, r=[sl[1](kf), ('RF', 4 + kf)], w=[('ps', bank)])
                op('dve', lambda e, nf=nf, bank=bank: e.tensor_tensor(out=xT[:, nf, :], in0=xT[:, nf, :], in1=PS[bank][:, :], op=ALU.add), r=[('xT', nf), ('ps', bank)], w=[('xT', nf)])
            done_slab(sl)

        for s_ in range(NSEQ):
            for n_ in range(NCHK):
                load_x(s_, n_)
                if DEPTH > 1:
                    rope_tables(s_, n_)
                for l in range(DEPTH):
                    ple_pre(l, s_, n_)
                    if l % 2 == 0:
                        conv_layer(l, s_, n_)
                    else:
                        attn_layer(l, s_, n_)
                    ple_full(l, s_, n_)
                nxt = (s_, n_ + 1) if n_ + 1 < NCHK else ((s_ + 1, 0) if s_ + 1 < NSEQ else None)
                if nxt is not None:
                    load_x_dma(nxt[0], nxt[1], 0)
                store_x(s_, n_)
        op('sp', None, r=out_keys)
        P.build()
        nc._prog_stats = (len(P.ops), P.nwaits, P.max_sem)
    return nc


_NC_CACHE = {}


def host_inputs(x, p, positions, norm_g, conv_w_in, conv_dw, conv_dw_b, conv_ln_g, conv_ln_b, conv_w_out, attn_w_in,
                attn_q_norm, attn_k_norm, attn_w_out, ple_w_proj, ple_norm_g, ple_w_gate, n_cores, nseq):
    f = lambda a: np.ascontiguousarray(np.asarray(a, dtype=np.float32))
    vecs = np.concatenate([f(norm_g), f(ple_norm_g), f(conv_dw_b), f(conv_ln_g), f(conv_ln_b), f(conv_dw)[0], f(conv_dw)[1]], axis=0)
    assert vecs.shape == (76, D)
    perm = np.arange(64)
    perm[0:8] = np.arange(8, 16)
    perm[8:16] = np.arange(0, 8)
    qn, kn = f(attn_q_norm), f(attn_k_norm)
    qkn = np.zeros((128, 8), np.float32)
    idx = np.arange(128) % 64
    for j in range(2):
        qkn[:, 4 * j + 0] = qn[j][idx]
        qkn[:, 4 * j + 1] = qn[j][perm][idx]
        qkn[:, 4 * j + 2] = kn[j][idx]
        qkn[:, 4 * j + 3] = kn[j][perm][idx]
    cst = make_consts()
    x = f(x)
    p = f(p)
    positions = np.ascontiguousarray(np.asarray(positions, dtype=np.int32))
    shared = {"vecs": vecs, "qkn": qkn, "cst": cst, "conv_w_in": f(conv_w_in), "conv_w_out": f(conv_w_out), "attn_w_in": f(attn_w_in),
              "attn_w_out": f(attn_w_out), "ple_w_proj": f(ple_w_proj), "ple_w_gate": f(ple_w_gate)}
    maps = []
    for c in range(n_cores):
        m = dict(shared)
        m["x"] = np.ascontiguousarray(x[c * nseq:(c + 1) * nseq])
        m["p"] = np.ascontiguousarray(p[:, c * nseq:(c + 1) * nseq])
        m["positions"] = np.ascontiguousarray(positions[c * nseq:(c + 1) * nseq])
        maps.append(m)
    return maps


def kernel(**inputs):
    n_cores = 8
    B, S, _ = inputs["x"].shape
    nseq = B // n_cores
    keyc = (nseq, S, 4)
    if keyc not in _NC_CACHE:
        _NC_CACHE[keyc] = build_nc(nseq, S, 4)
    nc = _NC_CACHE[keyc]
    maps = host_inputs(n_cores=n_cores, nseq=nseq, **inputs)
    res = run_bass_kernel_spmd(nc, maps, core_ids=list(range(n_cores)))
    return np.concatenate([np.asarray(r["out"]) for r in res.results], axis=0).astype(np.float32)
```
